# Optimizing a Trainium2 kernel written in Bass

```python
import math
import jax, jax.numpy as jnp
from jax import lax
import numpy as np

D_MODEL = 4096
BATCH = 4
SEQ = 2048
DEPTH = 2

CHUNK = 64
Q_BLOCK = 128
ATTN_HEAD_DIM = 128
ATTN_WIDTH = D_MODEL // 2
ATTN_HEADS = ATTN_WIDTH // ATTN_HEAD_DIM
SSM_WIDTH = D_MODEL // 2
SSM_GROUP = 16
SSM_GROUPS = SSM_WIDTH // SSM_GROUP
SSM_STATE = 64
D_FF = 4 * D_MODEL
PLE_DIM = 256
RMS_EPS = 1e-6
DT_MIN = 0.001
DT_MAX = 0.1
IN_WIDTH = 3 * ATTN_WIDTH + SSM_WIDTH + 2 * D_MODEL
SPLITS = [ATTN_WIDTH, 2 * ATTN_WIDTH, 3 * ATTN_WIDTH, 3 * ATTN_WIDTH + SSM_WIDTH,
          3 * ATTN_WIDTH + SSM_WIDTH + D_MODEL]

kernel_name = "hybrid_stickbreak_s5_gated_trunk"


def rmsnorm(x, g):
    xf = x.astype(jnp.float32)
    y = xf * lax.rsqrt(jnp.mean(xf * xf, axis=-1, keepdims=True) + RMS_EPS)
    return (y * g.astype(jnp.float32)).astype(x.dtype)


def stick_breaking_attention(q, k, v):
    b, s, h, dh = q.shape
    nb = s // Q_BLOCK
    scale = dh ** -0.5
    kf = k.astype(jnp.float32)
    vf = v.astype(jnp.float32)
    qb = q.astype(jnp.float32).reshape(b, nb, Q_BLOCK, h, dh).transpose(1, 0, 2, 3, 4)
    key_pos = jnp.arange(s)

    def block(args):
        q_blk, blk = args
        q_pos = blk * Q_BLOCK + jnp.arange(Q_BLOCK)
        mask = key_pos[None, :] < q_pos[:, None]
        z = jnp.einsum('bqhd,bkhd->bhqk', q_blk, kf) * scale
        log_beta = jax.nn.log_sigmoid(z)
        log_1m = jnp.where(mask, jax.nn.log_sigmoid(-z), 0.0)
        tail = lax.cumsum(log_1m, axis=3, reverse=True) - log_1m
        w = jnp.where(mask, jnp.exp(log_beta + tail), 0.0)
        return jnp.einsum('bhqk,bkhd->bqhd', w, vf)

    out = lax.map(block, (qb, jnp.arange(nb)))
    return out.transpose(1, 0, 2, 3, 4).reshape(b, s, h, dh).astype(q.dtype)


def s5_ssm(u, lam_re, lam_im, log_dt, b_re, b_im, c_re, c_im, d_skip):
    bsz, s, w = u.shape
    f32 = jnp.float32
    uf = u.astype(f32).reshape(bsz, s, SSM_GROUPS, SSM_GROUP)
    lam = lax.complex(lam_re.astype(f32), lam_im.astype(f32))
    dt = jnp.exp(log_dt.astype(f32))[:, None]
    lam_bar = jnp.exp(lam * dt)
    bmat = lax.complex(b_re.astype(f32), b_im.astype(f32))
    b_bar = ((lam_bar - 1.0) / lam)[..., None] * bmat
    cmat = lax.complex(c_re.astype(f32), c_im.astype(f32))
    bu = jnp.einsum('bsgc,gpc->sbgp', uf.astype(jnp.complex64), b_bar)
    a = jnp.broadcast_to(lam_bar[None, None], (s, 1, SSM_GROUPS, SSM_STATE))

    def combine(e1, e2):
        a1, x1 = e1
        a2, x2 = e2
        return a1 * a2, a2 * x1 + x2

    _, states = lax.associative_scan(combine, (a, bu), axis=0)
    y = jnp.einsum('sbgp,gcp->bsgc', states, cmat).real
    y = y + d_skip.astype(f32).reshape(SSM_GROUPS, SSM_GROUP) * uf
    return y.reshape(bsz, s, w).astype(u.dtype)


def setup_inputs(seed: int = 0) -> dict:
    key = jax.random.key(seed)
    ks = jax.random.split(key, 24)
    nrm = jax.random.normal
    L = DEPTH
    G, P, C = SSM_GROUPS, SSM_STATE, SSM_GROUP
    return {
        "x": nrm(ks[0], (BATCH, SEQ, D_MODEL), jnp.float32),
        "p": nrm(ks[1], (L, BATCH, SEQ, PLE_DIM), jnp.float32),
        "g_mix": 1.0 + 0.02 * nrm(ks[2], (L, D_MODEL), jnp.float32),
        "w_in": nrm(ks[3], (L, D_MODEL, IN_WIDTH), jnp.float32) * D_MODEL ** -0.5,
        "w_br_attn": nrm(ks[4], (L, ATTN_WIDTH, D_MODEL), jnp.float32) * ATTN_WIDTH ** -0.5,
        "lam_re": -0.5 + 0.01 * nrm(ks[5], (L, G, P), jnp.float32),
        "lam_im": math.pi * jnp.arange(P, dtype=jnp.float32)[None, None, :] + 0.01 * nrm(ks[6], (L, G, P), jnp.float32),
        "log_dt": jax.random.uniform(ks[7], (L, G), jnp.float32, math.log(DT_MIN), math.log(DT_MAX)),
        "b_re": nrm(ks[8], (L, G, P, C), jnp.float32) * (2.0 * C) ** -0.5,
        "b_im": nrm(ks[9], (L, G, P, C), jnp.float32) * (2.0 * C) ** -0.5,
        "c_re": nrm(ks[10], (L, G, C, P), jnp.float32) * (2.0 * P) ** -0.5,
        "c_im": nrm(ks[11], (L, G, C, P), jnp.float32) * (2.0 * P) ** -0.5,
        "d_skip": nrm(ks[12], (L, SSM_WIDTH), jnp.float32),
        "w_glu": nrm(ks[13], (L, SSM_WIDTH, SSM_WIDTH), jnp.float32) * SSM_WIDTH ** -0.5,
        "w_br_ssm": nrm(ks[14], (L, SSM_WIDTH, D_MODEL), jnp.float32) * SSM_WIDTH ** -0.5,
        "w_o": nrm(ks[15], (L, D_MODEL, D_MODEL), jnp.float32) * D_MODEL ** -0.5,
        "g_mlp": 1.0 + 0.02 * nrm(ks[16], (L, D_MODEL), jnp.float32),
        "w_ff1": nrm(ks[17], (L, D_MODEL, D_FF), jnp.float32) * D_MODEL ** -0.5,
        "w_ff2": nrm(ks[18], (L, D_FF, D_MODEL), jnp.float32) * D_FF ** -0.5,
        "g_ple": 1.0 + 0.02 * nrm(ks[19], (L, D_MODEL), jnp.float32),
        "w_ple_gate": nrm(ks[20], (L, D_MODEL, D_MODEL), jnp.float32) * D_MODEL ** -0.5,
        "w_ple": nrm(ks[21], (L, PLE_DIM, D_MODEL), jnp.float32) * PLE_DIM ** -0.5,
        "g_final": 1.0 + 0.02 * nrm(ks[22], (D_MODEL,), jnp.float32),
    }


def reference(x, p, g_mix, w_in, w_br_attn, lam_re, lam_im, log_dt, b_re, b_im,
              c_re, c_im, d_skip, w_glu, w_br_ssm, w_o, g_mlp, w_ff1, w_ff2,
              g_ple, w_ple_gate, w_ple, g_final):
    h = x
    bsz, s, _ = x.shape
    for i in range(DEPTH):
        xn = rmsnorm(h, g_mix[i])
        proj = xn @ w_in[i]
        q, k, v, u, g_a, g_s = jnp.split(proj, SPLITS, axis=-1)
        q = q.reshape(bsz, s, ATTN_HEADS, ATTN_HEAD_DIM)
        k = k.reshape(bsz, s, ATTN_HEADS, ATTN_HEAD_DIM)
        v = v.reshape(bsz, s, ATTN_HEADS, ATTN_HEAD_DIM)
        attn = stick_breaking_attention(q, k, v).reshape(bsz, s, ATTN_WIDTH)
        attn_up = attn @ w_br_attn[i]
        y = jax.nn.gelu(s5_ssm(u, lam_re[i], lam_im[i], log_dt[i], b_re[i], b_im[i],
                               c_re[i], c_im[i], d_skip[i]))
        ssm = y * jax.nn.sigmoid(y @ w_glu[i])
        ssm_up = ssm @ w_br_ssm[i]
        merged = jax.nn.sigmoid(g_a) * attn_up + jax.nn.sigmoid(g_s) * ssm_up
        h = h + merged @ w_o[i]
        hn = rmsnorm(h, g_mlp[i])
        h = h + jnp.square(jax.nn.relu(hn @ w_ff1[i])) @ w_ff2[i]
        gate = jax.nn.sigmoid(rmsnorm(h, g_ple[i]) @ w_ple_gate[i])
        h = h + (p[i] @ w_ple[i]) * gate
    return rmsnorm(h, g_final)
```

```python
import ml_dtypes
from concourse.bass_utils import run_bass_kernel_spmd
bf = ml_dtypes.bfloat16
import contextlib
import numpy as np
import concourse.bass as bass
import concourse.mybir as mybir

F32 = mybir.dt.float32
BF16 = mybir.dt.bfloat16
AF = mybir.ActivationFunctionType
ALU = mybir.AluOpType

ENG = {"pe": "tensor", "act": "scalar", "dve": "vector", "pool": "gpsimd", "sp": "sync"}


class Buf:
    __slots__ = ("name", "w", "r")

    def __init__(self, name):
        self.name = name
        self.w = None
        self.r = {}


class DSem:
    def __init__(self, handle):
        self.handle = handle
        self.count = 0


class Prog:
    def __init__(self, nc, stack):
        self.nc = nc
        self.stack = stack
        self.q = {e: [] for e in ENG}
        self.seq = {e: 0 for e in ENG}
        self.psem = {e: stack.enter_context(nc.semaphore("p_" + e)) for e in ENG if e != "sp"}
        self.known = {e: {} for e in ENG}
        self.dsems = []
        self.nbuf = 0

    def sb(self, name, shape, dt):
        return self.stack.enter_context(self.nc.sbuf_tensor(name, shape, dt))

    def psum(self, name, shape, dt):
        return self.stack.enter_context(self.nc.psum_tensor(name, shape, dt))

    def dsem(self, name):
        d = DSem(self.stack.enter_context(self.nc.semaphore(name)))
        self.dsems.append(d)
        return d

    def buf(self, name=None):
        self.nbuf += 1
        return Buf(name or "b%d" % self.nbuf)

    def _wait(self, eng, ev):
        if ev is None:
            return
        kind, key, val = ev
        if kind == "e":
            if key == eng and eng == "pe":
                return
            sem = self.psem[key]
            kid = ("e", key)
        else:
            sem = key.handle
            kid = ("d", id(key))
        if self.known[eng].get(kid, 0) >= val:
            return
        self.known[eng][kid] = val
        self.q[eng].append(("wait", sem, val))

    def _deps(self, eng, reads, writes):
        for b in reads:
            self._wait(eng, b.w)
        for b in writes:
            self._wait(eng, b.w)
            for kid, ev in b.r.items():
                self._wait(eng, ev)

    @staticmethod
    def _addr(b, ev):
        kid = (ev[0], ev[1] if ev[0] == "e" else id(ev[1]))
        old = b.r.get(kid)
        if old is None or old[2] < ev[2]:
            b.r[kid] = ev

    def op(self, eng, method, reads=(), writes=(), signal=True, **kw):
        self._deps(eng, reads, writes)
        ev = ("e", eng, self.seq[eng] + 1)
        if signal:
            self.seq[eng] += 1
        self.q[eng].append(("ins", method, kw, self.psem[eng] if signal else None, 1))
        for b in reads:
            self._addr(b, ev)
        for b in writes:
            b.w = ev
            b.r = {}

    def dma(self, eng, sem, out, in_, reads=(), writes=()):
        self._deps(eng, reads, writes)
        sem.count += 16
        ev = ("d", sem, sem.count)
        self.q[eng].append(("ins", "dma_start", dict(out=out, in_=in_), sem.handle, 16))
        for b in reads:
            self._addr(b, ev)
        for b in writes:
            b.w = ev
            b.r = {}

    def finalize(self):
        for d in self.dsems:
            if d.count:
                self._wait("sp", ("d", d, d.count))
        for e in ENG:
            if e != "sp" and self.seq[e]:
                self._wait("sp", ("e", e, self.seq[e]))
        nc = self.nc
        q = self.q

        def run(engobj, items):
            for it in items:
                if it[0] == "wait":
                    engobj.wait_ge(it[1], it[2])
                else:
                    _, method, kw, sem, inc = it
                    r = getattr(engobj, method)(**kw)
                    if sem is not None:
                        r.then_inc(sem, inc)

        with nc.Block() as block:
            @block.tensor
            def _(e):
                run(e, q["pe"])

            @block.scalar
            def _(e):
                run(e, q["act"])

            @block.vector
            def _(e):
                run(e, q["dve"])

            @block.gpsimd
            def _(e):
                run(e, q["pool"])

            @block.sync
            def _(e):
                run(e, q["sp"])


D = 4096
T = 1024
S = 2048
KC = D // 128
EPS = 1e-6
WSLOT = 8192
NWS = 3


class Ctx:
    def __init__(self, nc, stack):
        self.nc = nc
        self.P = P = Prog(nc, stack)
        self.ps = [P.psum("ps%d" % i, [128, 512], F32) for i in range(8)]
        self.psb = [P.buf("psb%d" % i) for i in range(8)]
        self.ws = [P.sb("ws%d" % i, [128, WSLOT], BF16) for i in range(NWS)]
        self.wsb = [P.buf("wsb%d" % i) for i in range(NWS)]
        self.wsem = [P.dsem("wsem%d" % i) for i in range(NWS)]
        self.wi = 0
        self.ones = P.sb("ones", [128, 128], BF16)
        self.onesb = P.buf("ones")
        P.op("dve", "memset", writes=[self.onesb], ap=self.ones[:], constant=1.0)
        self.onesf = P.sb("onesf", [128, 8], F32)
        self.onesfb = P.buf("onesf")
        P.op("dve", "memset", writes=[self.onesfb], ap=self.onesf[:], constant=1.0)
        self.epsc = P.sb("epsc", [128, 1], F32)
        self.epsb = P.buf("epsc")
        P.op("dve", "memset", writes=[self.epsb], ap=self.epsc[:], constant=EPS)
        self.bank_rr = 0

    def next_ws(self):
        i = self.wi % NWS
        self.wi += 1
        return i


def load_w(cx, W, r0, kc, c0, wc):
    P = cx.P
    i = cx.next_ws()
    view = cx.ws[i][:, 0:kc * wc].rearrange("p (k n) -> p k n", k=kc)
    src = W[r0:r0 + kc * 128, c0:c0 + wc].rearrange("(k p) n -> p k n", p=128)
    P.dma("pool", cx.wsem[i], out=view, in_=src, writes=[cx.wsb[i]])
    return view, cx.wsb[i]


def gemm_fm(cx, XT, XTb, kc, W, col0, ncols, epi, banks, r0=0, tk=T):
    P = cx.P
    wc = min(512, WSLOT // kc, ncols)
    nh = tk // 512
    jidx = 0
    bi = 0
    for c0 in range(col0, col0 + ncols, wc):
        wv, wb = load_w(cx, W, r0, kc, c0, wc)
        for jj in range(wc // 128):
            bset = banks[bi % len(banks)]
            bi += 1
            for h in range(nh):
                b = bset[h]
                for k in range(kc):
                    last = (k == kc - 1)
                    P.op("pe", "matmul", reads=[wb, XTb], writes=[cx.psb[b]],
                         signal=(last and h == nh - 1),
                         out=cx.ps[b][:, :], lhsT=wv[:, k, jj * 128:(jj + 1) * 128],
                         rhs=XT[:, k, h * 512:(h + 1) * 512], start=(k == 0), stop=last)
            epi(jidx, bset)
            jidx += 1


def prep_norm(cx, hT_dram, hT_b, g_sb, XT, XTb, st):
    P = cx.P
    ssq_banks = (6, 7)
    for fc in range(KC):
        s = fc % 2
        P.dma("sp", st["hsem"][s], out=st["hst"][:, s, :], in_=hT_dram[fc * 128:(fc + 1) * 128, :],
              reads=[hT_b], writes=[st["hstb"][s]])
        P.op("act", "activation", reads=[st["hstb"][s]], writes=[st["sqb"][s]],
             out=st["sq"][:, s, :], in_=st["hst"][:, s, :], func=AF.Square)
        P.op("dve", "tensor_scalar", reads=[st["hstb"][s], st["gb"]], writes=[XTb],
             out=XT[:, fc, :], in0=st["hst"][:, s, :], scalar1=g_sb[:, fc:fc + 1], scalar2=None,
             op0=ALU.mult)
        for h in range(2):
            P.op("pe", "matmul", reads=[st["sqb"][s], cx.onesb], writes=[cx.psb[ssq_banks[h]]],
                 signal=(h == 1),
                 out=cx.ps[ssq_banks[h]][:, :], lhsT=cx.ones[:, :], rhs=st["sq"][:, s, h * 512:(h + 1) * 512],
                 start=(fc == 0), stop=(fc == KC - 1))
    for h in range(2):
        P.op("act", "activation", reads=[cx.psb[ssq_banks[h]], cx.epsb], writes=[st["rstdb"]],
             out=st["rstd"][:, h * 512:(h + 1) * 512], in_=cx.ps[ssq_banks[h]][:, :], func=AF.Sqrt,
             scale=1.0 / D, bias=cx.epsc[:, 0:1])
    P.op("dve", "reciprocal", reads=[st["rstdb"]], writes=[st["rstdb"]],
         out=st["rstd"][:, :], in_=st["rstd"][:, :])


def alloc_norm_state(cx, name):
    P = cx.P
    st = {}
    st["hst"] = P.sb(name + "_hst", [128, 2, T], F32)
    st["hstb"] = [P.buf(), P.buf()]
    st["hsem"] = [P.dsem(name + "_hs0"), P.dsem(name + "_hs1")]
    st["sq"] = P.sb(name + "_sq", [128, 2, T], BF16)
    st["sqb"] = [P.buf(), P.buf()]
    st["rstd"] = P.sb(name + "_rstd", [128, T], F32)
    st["rstdb"] = P.buf()
    return st


def build_L1():
    nc = bass.Bass("TRN2", target_bir_lowering=False)
    hT = nc.dram_tensor("hT", [D, T], F32, kind="ExternalInput").ap()
    gmix = nc.dram_tensor("gmix", [128, KC], F32, kind="ExternalInput").ap()
    w_in = nc.dram_tensor("w_in", [D, 16384], F32, kind="ExternalInput").ap()
    qT = nc.dram_tensor("qT", [2048, T], BF16, kind="ExternalOutput").ap()
    kT = nc.dram_tensor("kT", [2048, T], BF16, kind="ExternalOutput").ap()
    vv = nc.dram_tensor("v", [T, 2048], BF16, kind="ExternalOutput").ap()
    uT = nc.dram_tensor("uT", [2048, T], F32, kind="ExternalOutput").ap()
    gT = nc.dram_tensor("gT", [8192, T], BF16, kind="ExternalOutput").ap()
    with contextlib.ExitStack() as stack:
        cx = Ctx(nc, stack)
        P = cx.P
        hTb = P.buf("hT")
        XT = P.sb("XT", [128, KC, T], BF16)
        XTb = P.buf("XT")
        st = alloc_norm_state(cx, "n1")
        g_sb = P.sb("g_sb", [128, KC], F32)
        st["gb"] = P.buf("g")
        gsem = P.dsem("gsem")
        P.dma("sp", gsem, out=g_sb[:, :], in_=gmix[:, :], writes=[st["gb"]])
        prep_norm(cx, hT, hTb, g_sb, XT, XTb, st)
        rstd, rstdb = st["rstd"], st["rstdb"]
        rtok = P.sb("rtok", [128, 8], F32)
        rtokb = P.buf("rtok")
        for tt in range(8):
            P.op("pe", "matmul", reads=[rstdb, cx.onesfb], writes=[cx.psb[5]], signal=(tt == 7),
                 out=cx.ps[5][:, tt:tt + 1], lhsT=rstd[0:1, tt * 128:(tt + 1) * 128], rhs=cx.onesf[0:1, 0:1],
                 start=True, stop=True)
        P.op("dve", "tensor_copy", reads=[cx.psb[5]], writes=[rtokb], out=rtok[:, :], in_=cx.ps[5][:, 0:8])

        ob16 = P.sb("ob16", [128, 3, T], BF16)
        ob16b = [P.buf() for _ in range(3)]
        ob32 = P.sb("ob32", [128, 2, T], F32)
        ob32b = [P.buf() for _ in range(2)]
        osem = [P.dsem("osem%d" % i) for i in range(3)]
        osem32 = [P.dsem("osem32_%d" % i) for i in range(2)]
        outb = P.buf("outs")
        cnt = {"o16": 0, "o32": 0}
        banks = [(0, 1), (2, 3)]
        SCALE = 128.0 ** -0.5

        def epi_qk(dst, scale):
            def epi(j, bset):
                s = cnt["o16"] % 3
                cnt["o16"] += 1
                for h in range(2):
                    if scale is not None:
                        P.op("dve", "scalar_tensor_tensor", reads=[cx.psb[bset[h]], rstdb], writes=[ob16b[s]],
                             out=ob16[:, s, h * 512:(h + 1) * 512], in0=cx.ps[bset[h]][:, :], scalar=scale,
                             in1=rstd[:, h * 512:(h + 1) * 512], op0=ALU.mult, op1=ALU.mult)
                    else:
                        P.op("dve", "tensor_tensor", reads=[cx.psb[bset[h]], rstdb], writes=[ob16b[s]],
                             out=ob16[:, s, h * 512:(h + 1) * 512], in0=cx.ps[bset[h]][:, :],
                             in1=rstd[:, h * 512:(h + 1) * 512], op=ALU.mult)
                P.dma("sp", osem[s], out=dst[j * 128:(j + 1) * 128, :], in_=ob16[:, s, :], reads=[ob16b[s]],
                      writes=[])
            return epi

        def epi_u(j, bset):
            s = cnt["o32"] % 2
            cnt["o32"] += 1
            for h in range(2):
                P.op("dve", "tensor_tensor", reads=[cx.psb[bset[h]], rstdb], writes=[ob32b[s]],
                     out=ob32[:, s, h * 512:(h + 1) * 512], in0=cx.ps[bset[h]][:, :],
                     in1=rstd[:, h * 512:(h + 1) * 512], op=ALU.mult)
            P.dma("sp", osem32[s], out=uT[j * 128:(j + 1) * 128, :], in_=ob32[:, s, :], reads=[ob32b[s]])

        def epi_g(j, bset):
            s32 = cnt["o32"] % 2
            cnt["o32"] += 1
            s = cnt["o16"] % 3
            cnt["o16"] += 1
            for h in range(2):
                P.op("dve", "tensor_tensor", reads=[cx.psb[bset[h]], rstdb], writes=[ob32b[s32]],
                     out=ob32[:, s32, h * 512:(h + 1) * 512], in0=cx.ps[bset[h]][:, :],
                     in1=rstd[:, h * 512:(h + 1) * 512], op=ALU.mult)
            P.op("act", "activation", reads=[ob32b[s32]], writes=[ob16b[s]],
                 out=ob16[:, s, :], in_=ob32[:, s32, :], func=AF.Sigmoid)
            P.dma("sp", osem[s], out=gT[j * 128:(j + 1) * 128, :], in_=ob16[:, s, :], reads=[ob16b[s]])

        gemm_fm(cx, XT, XTb, KC, w_in, 0, 2048, epi_qk(qT, SCALE), banks)
        gemm_fm(cx, XT, XTb, KC, w_in, 2048, 2048, epi_qk(kT, None), banks)
        wc = WSLOT // KC
        vb16 = P.sb("vb16", [128, 2, 256], BF16)
        vbb = [P.buf(), P.buf()]
        vsem = [P.dsem("vsem0"), P.dsem("vsem1")]
        vi = 0
        for c0 in range(0, 2048, wc):
            wv, wb = load_w(cx, w_in, 0, KC, 4096 + c0, wc)
            for tt in range(8):
                b = (0, 1, 2, 3)[vi % 4]
                for k in range(KC):
                    P.op("pe", "matmul", reads=[wb, XTb], writes=[cx.psb[b]], signal=(k == KC - 1),
                         out=cx.ps[b][:, 0:wc], lhsT=XT[:, k, tt * 128:(tt + 1) * 128], rhs=wv[:, k, :],
                         start=(k == 0), stop=(k == KC - 1))
                s = vi % 2
                vi += 1
                P.op("dve", "tensor_scalar", reads=[cx.psb[b], rtokb], writes=[vbb[s]],
                     out=vb16[:, s, :], in0=cx.ps[b][:, 0:wc], scalar1=rtok[:, tt:tt + 1], scalar2=None,
                     op0=ALU.mult)
                P.dma("sp", vsem[s], out=vv[tt * 128:(tt + 1) * 128, c0:c0 + wc], in_=vb16[:, s, :],
                      reads=[vbb[s]])
        gemm_fm(cx, XT, XTb, KC, w_in, 6144, 2048, epi_u, banks)
        gemm_fm(cx, XT, XTb, KC, w_in, 8192, 8192, epi_g, banks)
        P.finalize()
    return nc


def build_L3(final):
    nc = bass.Bass("TRN2", target_bir_lowering=False)
    dt_ = nc.dram_tensor
    attnT = dt_("attnT", [2048, T], BF16, kind="ExternalInput").ap()
    yT = dt_("yT", [2048, T], F32, kind="ExternalInput").ap()
    gT = dt_("gT", [8192, T], BF16, kind="ExternalInput").ap()
    hT = dt_("hT", [D, T], F32, kind="ExternalInput").ap()
    pT = dt_("pT", [256, T], F32, kind="ExternalInput").ap()
    w_bra = dt_("w_bra", [2048, D], F32, kind="ExternalInput").ap()
    w_glu = dt_("w_glu", [2048, 2048], F32, kind="ExternalInput").ap()
    w_brs = dt_("w_brs", [2048, D], F32, kind="ExternalInput").ap()
    w_o = dt_("w_o", [D, D], F32, kind="ExternalInput").ap()
    w_ff1 = dt_("w_ff1", [D, 16384], F32, kind="ExternalInput").ap()
    w_ff2 = dt_("w_ff2", [16384, D], F32, kind="ExternalInput").ap()
    w_pg = dt_("w_pg", [D, D], F32, kind="ExternalInput").ap()
    w_ple = dt_("w_ple", [256, D], F32, kind="ExternalInput").ap()
    gvec = dt_("gvec", [128, 3 * KC], F32, kind="ExternalInput").ap()
    hout = dt_("hout", [D, T], F32, kind="ExternalOutput").ap()
    h2 = dt_("h2", [D, T], F32).ap()
    h3 = dt_("h3", [D, T], F32).ap()
    h4 = dt_("h4", [D, T], F32).ap()
    aT = dt_("aT", [16384, T], BF16).ap()
    with contextlib.ExitStack() as stack:
        cx = Ctx(nc, stack)
        P = cx.P
        R1 = P.sb("R1", [128, KC, T], BF16)
        R2 = P.sb("R2", [128, KC, T], BF16)
        R1a, R1b, R2b = P.buf("R1a"), P.buf("R1b"), P.buf("R2")
        g_sb = P.sb("g_sb", [128, 3 * KC], F32)
        gb = P.buf("g")
        msem = P.dsem("msem")
        P.dma("sp", msem, out=g_sb[:, :], in_=gvec[:, :], writes=[gb])
        st32 = P.sb("st32", [128, 3, T], F32)
        st32b = [P.buf() for _ in range(3)]
        st32s = [P.dsem("st32s%d" % i) for i in range(3)]
        st16 = P.sb("st16", [128, 3, T], BF16)
        st16b = [P.buf() for _ in range(3)]
        st16s = [P.dsem("st16s%d" % i) for i in range(3)]
        rstd = P.sb("rstd", [128, T], F32)
        rstdb = P.buf("rstd")
        ssqa = P.sb("ssqa", [128, T], F32)
        ssqab = P.buf("ssqa")
        c32 = [0]
        c16 = [0]

        def n32():
            c32[0] += 1
            return (c32[0] - 1) % 3

        def n16():
            c16[0] += 1
            return (c16[0] - 1) % 3

        banks = [(0, 1), (2, 3)]
        h2b = [P.buf() for _ in range(KC)]
        h3b = [P.buf() for _ in range(KC)]
        h4b = [P.buf() for _ in range(KC)]
        aTb = [P.buf() for _ in range(128)]

        for c in range(16):
            s = n32()
            s2 = n32()
            P.dma("sp", st32s[s], out=st32[:, s, :], in_=yT[c * 128:(c + 1) * 128, :], writes=[st32b[s]])
            P.op("act", "activation", reads=[st32b[s]], writes=[st32b[s2]],
                 out=st32[:, s2, :], in_=st32[:, s, :], func=AF.Square)
            P.op("dve", "tensor_scalar", reads=[st32b[s2]], writes=[st32b[s2]],
                 out=st32[:, s2, :], in0=st32[:, s2, :], scalar1=0.044715, scalar2=1.0, op0=ALU.mult, op1=ALU.add)
            P.op("dve", "tensor_tensor", reads=[st32b[s2], st32b[s]], writes=[st32b[s2]],
                 out=st32[:, s2, :], in0=st32[:, s2, :], in1=st32[:, s, :], op=ALU.mult)
            P.op("act", "activation", reads=[st32b[s2]], writes=[st32b[s2]],
                 out=st32[:, s2, :], in_=st32[:, s2, :], func=AF.Sigmoid, scale=1.5957691216057308)
            P.op("dve", "tensor_tensor", reads=[st32b[s2], st32b[s]], writes=[R1a],
                 out=R1[:, c, :], in0=st32[:, s2, :], in1=st32[:, s, :], op=ALU.mult)

        def epi_glu(j, bset):
            s = n32()
            for h in range(2):
                P.op("act", "activation", reads=[cx.psb[bset[h]]], writes=[st32b[s]],
                     out=st32[:, s, h * 512:(h + 1) * 512], in_=cx.ps[bset[h]][:, :], func=AF.Sigmoid)
            P.op("dve", "tensor_tensor", reads=[st32b[s], R1a], writes=[R1b],
                 out=R1[:, 16 + j, :], in0=st32[:, s, :], in1=R1[:, j, :], op=ALU.mult)

        gemm_fm(cx, R1[:, 0:16, :], R1a, 16, w_glu, 0, 2048, epi_glu, banks)

        atsem = P.dsem("atsem")
        for c in range(16):
            P.dma("sp", atsem, out=R1[:, c, :], in_=attnT[c * 128:(c + 1) * 128, :], writes=[R1a])

        for j0 in range(0, KC, 4):
            wa, wab = load_w(cx, w_bra, 0, 16, j0 * 128, 512)
            wsv, wsb_ = load_w(cx, w_brs, 0, 16, j0 * 128, 512)
            for jj in range(4):
                j = j0 + jj
                for (wv, wb, xoff, xb, bset) in ((wa, wab, 0, R1a, (0, 1)), (wsv, wsb_, 16, R1b, (2, 3))):
                    for h in range(2):
                        for k in range(16):
                            P.op("pe", "matmul", reads=[wb, xb], writes=[cx.psb[bset[h]]],
                                 signal=(k == 15 and h == 1),
                                 out=cx.ps[bset[h]][:, :], lhsT=wv[:, k, jj * 128:(jj + 1) * 128],
                                 rhs=R1[:, xoff + k, h * 512:(h + 1) * 512], start=(k == 0), stop=(k == 15))
                sa, sg = n16(), n16()
                P.dma("sp", st16s[sa], out=st16[:, sa, :], in_=gT[j * 128:(j + 1) * 128, :], writes=[st16b[sa]])
                P.dma("sp", st16s[sg], out=st16[:, sg, :], in_=gT[(32 + j) * 128:(33 + j) * 128, :],
                      writes=[st16b[sg]])
                s1, s2 = n32(), n32()
                for h in range(2):
                    hs = slice(h * 512, (h + 1) * 512)
                    P.op("dve", "tensor_tensor", reads=[cx.psb[0 + h], st16b[sa]], writes=[st32b[s1]],
                         out=st32[:, s1, hs], in0=cx.ps[0 + h][:, :], in1=st16[:, sa, hs], op=ALU.mult)
                    P.op("dve", "tensor_tensor", reads=[cx.psb[2 + h], st16b[sg]], writes=[st32b[s2]],
                         out=st32[:, s2, hs], in0=cx.ps[2 + h][:, :], in1=st16[:, sg, hs], op=ALU.mult)
                P.op("dve", "tensor_tensor", reads=[st32b[s1], st32b[s2]], writes=[R2b],
                     out=R2[:, j, :], in0=st32[:, s1, :], in1=st32[:, s2, :], op=ALU.add)

        sq = P.sb("sq", [128, 2, T], BF16)
        sqb = [P.buf(), P.buf()]
        sqc = [0]

        def make_rstd(src_is_psum, ssq_banks=None):
            for h in range(2):
                hs = slice(h * 512, (h + 1) * 512)
                if src_is_psum:
                    P.op("act", "activation", reads=[cx.psb[ssq_banks[h]], cx.epsb], writes=[rstdb],
                         out=rstd[:, hs], in_=cx.ps[ssq_banks[h]][:, :], func=AF.Sqrt, scale=1.0 / D,
                         bias=cx.epsc[:, 0:1])
                else:
                    P.op("act", "activation", reads=[ssqab, cx.epsb], writes=[rstdb],
                         out=rstd[:, hs], in_=ssqa[:, hs], func=AF.Sqrt, scale=1.0 / D, bias=cx.epsc[:, 0:1])
            P.op("dve", "reciprocal", reads=[rstdb], writes=[rstdb], out=rstd[:, :], in_=rstd[:, :])

        def epi_wo(j, bset):
            s = n32()
            P.dma("sp", st32s[s], out=st32[:, s, :], in_=hT[j * 128:(j + 1) * 128, :], writes=[st32b[s]])
            for h in range(2):
                hs = slice(h * 512, (h + 1) * 512)
                P.op("dve", "tensor_tensor", reads=[cx.psb[bset[h]], st32b[s]], writes=[st32b[s]],
                     out=st32[:, s, hs], in0=cx.ps[bset[h]][:, :], in1=st32[:, s, hs], op=ALU.add)
            P.dma("sp", st32s[s], out=h2[j * 128:(j + 1) * 128, :], in_=st32[:, s, :],
                  reads=[st32b[s]], writes=[h2b[j]])
            q = sqc[0] % 2
            sqc[0] += 1
            P.op("act", "activation", reads=[st32b[s]], writes=[sqb[q]],
                 out=sq[:, q, :], in_=st32[:, s, :], func=AF.Square)
            P.op("dve", "tensor_scalar", reads=[st32b[s], gb], writes=[R1a, R1b],
                 out=R1[:, j, :], in0=st32[:, s, :], scalar1=g_sb[:, j:j + 1], scalar2=None, op0=ALU.mult)
            for h in range(2):
                P.op("pe", "matmul", reads=[sqb[q], cx.onesb], writes=[cx.psb[6 + h]],
                     signal=(h == 1), out=cx.ps[6 + h][:, :], lhsT=cx.ones[:, :],
                     rhs=sq[:, q, h * 512:(h + 1) * 512], start=(j == 0), stop=(j == KC - 1))

        gemm_fm(cx, R2, R2b, KC, w_o, 0, D, epi_wo, banks)
        make_rstd(True, (6, 7))

        def epi_ff1(f, bset):
            s = n32()
            for h in range(2):
                hs = slice(h * 512, (h + 1) * 512)
                P.op("dve", "scalar_tensor_tensor", reads=[cx.psb[bset[h]], rstdb], writes=[st32b[s]],
                     out=st32[:, s, hs], in0=cx.ps[bset[h]][:, :], scalar=0.0, in1=rstd[:, hs],
                     op0=ALU.max, op1=ALU.mult)
            s16 = n16()
            P.op("act", "activation", reads=[st32b[s]], writes=[st16b[s16]],
                 out=st16[:, s16, :], in_=st32[:, s, :], func=AF.Square)
            P.dma("sp", st16s[s16], out=aT[f * 128:(f + 1) * 128, :], in_=st16[:, s16, :],
                  reads=[st16b[s16]], writes=[aTb[f]])

        def gemm_fm2(XT, xbufs, kc, W, col0, ncols, epi):
            wc = min(512, WSLOT // kc, ncols)
            jidx = 0
            bi = 0
            for c0 in range(col0, col0 + ncols, wc):
                wv, wb = load_w(cx, W, 0, kc, c0, wc)
                for jj in range(wc // 128):
                    bset = banks[bi % 2]
                    bi += 1
                    for h in range(2):
                        for k in range(kc):
                            last = (k == kc - 1)
                            P.op("pe", "matmul", reads=[wb] + xbufs, writes=[cx.psb[bset[h]]],
                                 signal=(last and h == 1), out=cx.ps[bset[h]][:, :],
                                 lhsT=wv[:, k, jj * 128:(jj + 1) * 128], rhs=XT[:, k, h * 512:(h + 1) * 512],
                                 start=(k == 0), stop=last)
                    epi(jidx, bset)
                    jidx += 1

        gemm_fm2(R1, [R1a, R1b], KC, w_ff1, 0, 16384, epi_ff1)

        aslot_b = [P.buf() for _ in range(4)]
        aslot_s = [P.dsem("asl%d" % i) for i in range(4)]
        ai = [0]
        first_ssq = [True]
        for ps_ in range(8):
            for fg in range(8):
                wv, wb = load_w(cx, w_ff2, fg * 2048, 16, ps_ * 512, 512)
                for f4 in range(4):
                    sl = ai[0] % 4
                    ai[0] += 1
                    f0 = fg * 16 + f4 * 4
                    extra = [R1a, R1b] if (ps_ == 0 and fg == 0 and f4 < 4 and ai[0] <= 4) else []
                    P.dma("sp", aslot_s[sl], out=R1[:, sl * 4:(sl + 1) * 4, :],
                          in_=aT[f0 * 128:(f0 + 4) * 128, :].rearrange("(c p) t -> p c t", p=128),
                          reads=[aTb[f0 + i] for i in range(4)], writes=[aslot_b[sl]] + extra)
                    for fi in range(4):
                        f = f0 + fi
                        kf = f4 * 4 + fi
                        for jj in range(4):
                            for h in range(2):
                                b = jj * 2 + h
                                lastf = (f == 127)
                                P.op("pe", "matmul", reads=[wb, aslot_b[sl]], writes=[cx.psb[b]],
                                     signal=(jj == 3 and h == 1 and (fi == 3 or lastf)),
                                     out=cx.ps[b][:, :], lhsT=wv[:, kf, jj * 128:(jj + 1) * 128],
                                     rhs=R1[:, sl * 4 + fi, h * 512:(h + 1) * 512], start=(f == 0), stop=lastf)
            qs = []
            for jj in range(4):
                j = ps_ * 4 + jj
                s = n32()
                P.dma("sp", st32s[s], out=st32[:, s, :], in_=h2[j * 128:(j + 1) * 128, :],
                      reads=[h2b[j]], writes=[st32b[s]])
                for h in range(2):
                    hs = slice(h * 512, (h + 1) * 512)
                    P.op("dve", "tensor_tensor", reads=[cx.psb[jj * 2 + h], st32b[s]], writes=[st32b[s]],
                         out=st32[:, s, hs], in0=cx.ps[jj * 2 + h][:, :], in1=st32[:, s, hs], op=ALU.add)
                P.dma("sp", st32s[s], out=h3[j * 128:(j + 1) * 128, :], in_=st32[:, s, :],
                      reads=[st32b[s]], writes=[h3b[j]])
                P.op("dve", "tensor_scalar", reads=[st32b[s], gb], writes=[R2b],
                     out=R2[:, j, :], in0=st32[:, s, :], scalar1=g_sb[:, KC + j:KC + j + 1], scalar2=None,
                     op0=ALU.mult)
                q = sqc[0] % 2
                sqc[0] += 1
                P.op("act", "activation", reads=[st32b[s]], writes=[sqb[q]],
                     out=sq[:, q, :], in_=st32[:, s, :], func=AF.Square)
                qs.append(q)
                for h in range(2):
                    P.op("pe", "matmul", reads=[sqb[q], cx.onesb], writes=[cx.psb[h]],
                         signal=(h == 1), out=cx.ps[h][:, :], lhsT=cx.ones[:, :],
                         rhs=sq[:, q, h * 512:(h + 1) * 512], start=(jj == 0), stop=(jj == 3))
            for h in range(2):
                hs = slice(h * 512, (h + 1) * 512)
                if ps_ == 0:
                    P.op("dve", "tensor_copy", reads=[cx.psb[h]], writes=[ssqab], out=ssqa[:, hs], in_=cx.ps[h][:, :])
                else:
                    P.op("dve", "tensor_tensor", reads=[cx.psb[h], ssqab], writes=[ssqab],
                         out=ssqa[:, hs], in0=cx.ps[h][:, :], in1=ssqa[:, hs], op=ALU.add)
        make_rstd(False)

        PT = st16[:, 0:2, :]
        PTb = st16b[0]
        ptsem = P.dsem("ptsem")
        P.dma("pool", ptsem, out=PT, in_=pT.rearrange("(c p) t -> p c t", p=128), writes=[st16b[0], st16b[1]])
        hdst = h4 if final else hout
        pj = [0]

        def epi_ple(j, bset):
            jj = j % 4
            if jj == 0:
                wple_cur[0] = load_w(cx, w_ple, 0, 2, j * 128, 512)
            wv, wb = wple_cur[0]
            for h in range(2):
                for k in range(2):
                    P.op("pe", "matmul", reads=[wb, PTb], writes=[cx.psb[4 + h]], signal=(k == 1 and h == 1),
                         out=cx.ps[4 + h][:, :], lhsT=wv[:, k, jj * 128:(jj + 1) * 128],
                         rhs=PT[:, k, h * 512:(h + 1) * 512], start=(k == 0), stop=(k == 1))
            s, s2 = n32(), n32()
            for h in range(2):
                hs = slice(h * 512, (h + 1) * 512)
                P.op("dve", "tensor_tensor", reads=[cx.psb[bset[h]], rstdb], writes=[st32b[s]],
                     out=st32[:, s, hs], in0=cx.ps[bset[h]][:, :], in1=rstd[:, hs], op=ALU.mult)
            P.op("act", "activation", reads=[st32b[s]], writes=[st32b[s]],
                 out=st32[:, s, :], in_=st32[:, s, :], func=AF.Sigmoid)
            for h in range(2):
                hs = slice(h * 512, (h + 1) * 512)
                P.op("dve", "tensor_tensor", reads=[cx.psb[4 + h], st32b[s]], writes=[st32b[s]],
                     out=st32[:, s, hs], in0=cx.ps[4 + h][:, :], in1=st32[:, s, hs], op=ALU.mult)
            P.dma("sp", st32s[s2], out=st32[:, s2, :], in_=h3[j * 128:(j + 1) * 128, :],
                  reads=[h3b[j]], writes=[st32b[s2]])
            P.op("dve", "tensor_tensor", reads=[st32b[s], st32b[s2]], writes=[st32b[s2]],
                 out=st32[:, s2, :], in0=st32[:, s, :], in1=st32[:, s2, :], op=ALU.add)
            P.dma("sp", st32s[s2], out=hdst[j * 128:(j + 1) * 128, :], in_=st32[:, s2, :],
                  reads=[st32b[s2]], writes=[h4b[j]])
            if final:
                q = sqc[0] % 2
                sqc[0] += 1
                P.op("act", "activation", reads=[st32b[s2]], writes=[sqb[q]],
                     out=sq[:, q, :], in_=st32[:, s2, :], func=AF.Square)
                for h in range(2):
                    P.op("pe", "matmul", reads=[sqb[q], cx.onesb], writes=[cx.psb[6 + h]],
                         signal=(h == 1), out=cx.ps[6 + h][:, :], lhsT=cx.ones[:, :],
                         rhs=sq[:, q, h * 512:(h + 1) * 512], start=(j == 0), stop=(j == KC - 1))

        wple_cur = [None]
        gemm_fm(cx, R2, R2b, KC, w_pg, 0, D, epi_ple, banks)
        if final:
            make_rstd(True, (6, 7))
            for j in range(KC):
                s = n32()
                P.dma("sp", st32s[s], out=st32[:, s, :], in_=h4[j * 128:(j + 1) * 128, :],
                      reads=[h4b[j]], writes=[st32b[s]])
                P.op("dve", "scalar_tensor_tensor", reads=[st32b[s], rstdb, gb], writes=[st32b[s]],
                     out=st32[:, s, :], in0=st32[:, s, :], scalar=g_sb[:, 2 * KC + j:2 * KC + j + 1],
                     in1=rstd[:, :], op0=ALU.mult, op1=ALU.mult)
                P.dma("sp", st32s[s], out=hout[j * 128:(j + 1) * 128, :], in_=st32[:, s, :], reads=[st32b[s]])
        P.finalize()
    return nc


def build_L2():
    import math
    nc = bass.Bass("TRN2", target_bir_lowering=False)
    dt_ = nc.dram_tensor
    qT = dt_("qT", [2048, T], BF16, kind="ExternalInput").ap()
    kTf = dt_("kTf", [2048, S], BF16, kind="ExternalInput").ap()
    vhm = dt_("vhm", [16, 128, 16, 128], BF16, kind="ExternalInput").ap()
    maskd = dt_("mask", [128, 128], F32, kind="ExternalInput").ap()
    identd = dt_("ident", [128, 128], BF16, kind="ExternalInput").ap()
    uTg = dt_("uTg", [1024, S], F32, kind="ExternalInput").ap()
    lamd = dt_("lam", [128, 96], F32, kind="ExternalInput").ap()
    BTd = dt_("BT", [32, 2, 32, 128], F32, kind="ExternalInput").ap()
    Cd = dt_("Cm", [128, 2, 32, 32], F32, kind="ExternalInput").ap()
    dskd = dt_("dsk", [32, 32], F32, kind="ExternalInput").ap()
    attnT = dt_("attnT", [2048, T], BF16, kind="ExternalOutput").ap()
    yg = dt_("yg", [1024, S], F32, kind="ExternalOutput").ap()
    with contextlib.ExitStack() as stack:
        P = Prog(nc, stack)
        ps = [P.psum("ps%d" % i, [128, 512], F32) for i in range(4)]
        pT = [P.psum("pT%d" % i, [128, 1024], BF16) for i in range(2)]
        ps6 = P.psum("ps6", [128, 512], F32)
        psb = [P.buf() for _ in range(4)]
        pTb = [P.buf(), P.buf()]
        ps6b = P.buf()
        csem = P.dsem("csem")
        mask = P.sb("mask_sb", [128, 128], F32); maskb = P.buf()
        ident = P.sb("ident_sb", [128, 128], BF16); identb = P.buf()
        P.dma("sp", csem, out=mask[:, :], in_=maskd[:, :], writes=[maskb])
        csem_i = P.dsem("csem_i")
        P.dma("sp", csem_i, out=ident[:, :], in_=identd[:, :], writes=[identb])
        ones32 = P.sb("ones32", [128, S], F32); ones32b = P.buf()
        P.op("dve", "memset", writes=[ones32b], ap=ones32[:, :], constant=1.0)

        lam = P.sb("lam_sb", [128, 96], F32); lamb = P.buf()
        csem_l = P.dsem("csem_l")
        P.dma("sp", csem_l, out=lam[:, :], in_=lamd[:, :], writes=[lamb])
        BT = P.sb("BT_sb", [32, 2, 32, 128], BF16); BTb = P.buf()
        bsem = P.dsem("bsem")
        P.dma("pool", bsem, out=BT[:, :, :, :], in_=BTd[:, :, :, :], writes=[BTb])
        Cm = P.sb("Cm_sb", [128, 2, 32, 32], F32); Cmb = P.buf()
        c2sem = P.dsem("c2sem")
        P.dma("sp", c2sem, out=Cm[:, :, :, :], in_=Cd[:, :, :, :], writes=[Cmb])
        dsk = P.sb("dsk_sb", [32, 32], F32); dskb = P.buf()
        d2sem = P.dsem("d2sem")
        P.dma("sp", d2sem, out=dsk[:, :], in_=dskd[:, :], writes=[dskb])
        pr = P.sb("pr", [128, 16, 32], F32); prb = P.buf()
        LR, LI, LD = lam[:, 0:32], lam[:, 32:64], lam[:, 64:96]
        X = lambda k: pr[:, k, :]
        def dv(method, **kw):
            P.op("dve", method, reads=[prb, lamb], writes=[prb], **kw)
        def ac(**kw):
            P.op("act", "activation", reads=[prb, lamb], writes=[prb], **kw)
        TWO_PI = 2.0 * math.pi
        MAGIC = 12582912.0
        ac(out=X(0), in_=LD, func=AF.Exp)
        dv("tensor_tensor", out=X(1), in0=LR, in1=X(0), op=ALU.mult)
        ac(out=X(2), in_=X(1), func=AF.Exp)
        dv("tensor_tensor", out=X(3), in0=LI, in1=X(0), op=ALU.mult)
        dv("tensor_scalar", out=X(3), in0=X(3), scalar1=1.0 / TWO_PI, scalar2=None, op0=ALU.mult)
        dv("tensor_scalar", out=X(4), in0=X(3), scalar1=MAGIC, scalar2=None, op0=ALU.add)
        dv("tensor_scalar", out=X(4), in0=X(4), scalar1=MAGIC, scalar2=None, op0=ALU.subtract)
        dv("tensor_tensor", out=X(4), in0=X(3), in1=X(4), op=ALU.subtract)
        ac(out=X(5), in_=X(4), func=AF.Sin, scale=TWO_PI)
        dv("tensor_scalar", out=X(6), in0=X(3), scalar1=0.25, scalar2=None, op0=ALU.add)
        dv("tensor_scalar", out=X(4), in0=X(6), scalar1=MAGIC, scalar2=None, op0=ALU.add)
        dv("tensor_scalar", out=X(4), in0=X(4), scalar1=MAGIC, scalar2=None, op0=ALU.subtract)
        dv("tensor_tensor", out=X(4), in0=X(6), in1=X(4), op=ALU.subtract)
        ac(out=X(6), in_=X(4), func=AF.Sin, scale=TWO_PI)
        COS, SIN, RHO = X(6), X(5), X(2)
        dv("tensor_tensor", out=X(7), in0=RHO, in1=COS, op=ALU.mult)
        dv("tensor_scalar", out=X(7), in0=X(7), scalar1=-1.0, scalar2=None, op0=ALU.add)
        dv("tensor_tensor", out=X(8), in0=RHO, in1=SIN, op=ALU.mult)
        dv("tensor_tensor", out=X(9), in0=LR, in1=LR, op=ALU.mult)
        dv("tensor_tensor", out=X(10), in0=LI, in1=LI, op=ALU.mult)
        dv("tensor_tensor", out=X(9), in0=X(9), in1=X(10), op=ALU.add)
        dv("reciprocal", out=X(9), in_=X(9))
        dv("tensor_tensor", out=X(10), in0=X(7), in1=LR, op=ALU.mult)
        dv("tensor_tensor", out=X(11), in0=X(8), in1=LI, op=ALU.mult)
        dv("tensor_tensor", out=X(10), in0=X(10), in1=X(11), op=ALU.add)
        dv("tensor_tensor", out=X(10), in0=X(10), in1=X(9), op=ALU.mult)
        dv("tensor_tensor", out=X(11), in0=X(8), in1=LR, op=ALU.mult)
        dv("tensor_tensor", out=X(12), in0=X(7), in1=LI, op=ALU.mult)
        dv("tensor_tensor", out=X(11), in0=X(11), in1=X(12), op=ALU.subtract)
        dv("tensor_tensor", out=X(11), in0=X(11), in1=X(9), op=ALU.mult)
        dv("tensor_scalar", out=X(12), in0=X(11), scalar1=-1.0, scalar2=None, op0=ALU.mult)
        KRE, KIM, NKIM = X(10), X(11), X(12)

        Ec = P.sb("Ec", [128, S], F32); Es = P.sb("Es", [128, S], F32); Eb = P.buf()
        rhoT = P.sb("rhoT", [128, T], F32); rhoTb = P.buf()
        tmpA = P.sb("tmpA", [128, T], F32); tmpAb = P.buf()
        tmpB = P.sb("tmpB", [128, T], F32); tmpBb = P.buf()
        Zr = P.sb("Zr", [128, 2, T], F32); Zi = P.sb("Zi", [128, 2, T], F32); Zrb = [P.buf(), P.buf()]; Zib = [P.buf(), P.buf()]
        xr = P.sb("xr", [128, T], BF16); xi = P.sb("xi", [128, T], BF16); xrb = P.buf(); xib = P.buf()
        ut = P.sb("ut", [32, 2, S], F32); utb = [P.buf(), P.buf()]; utsem = [P.dsem("uts0"), P.dsem("uts1")]
        ub = P.sb("ub", [32, 2, S], BF16); ubb = [P.buf(), P.buf()]
        Ct = P.sb("Ct", [128, 2, 2, 32], BF16); Ctb = [P.buf(), P.buf()]
        ctmp = P.sb("ctmp", [128, 32], F32); ctmpb = P.buf()
        yo = P.sb("yo", [32, 2, S], F32); yob = [P.buf(), P.buf()]; yosem = [P.dsem("yos0"), P.dsem("yos1")]
        for i in range(32):
            s = i % 2
            P.dma("sp", utsem[s], out=ut[:, s, :], in_=uTg[i * 32:(i + 1) * 32, :], writes=[utb[s]])
            P.op("act", "activation", reads=[utb[s]], writes=[ubb[s]], out=ub[:, s, :], in_=ut[:, s, :], func=AF.Copy)
            P.op("dve", "tensor_scalar", reads=[Cmb, prb], writes=[ctmpb], out=ctmp[:, :], in0=Cm[:, 0, i, :],
                 scalar1=KRE[:, i:i + 1], scalar2=None, op0=ALU.mult)
            P.op("dve", "scalar_tensor_tensor", reads=[Cmb, prb, ctmpb], writes=[Ctb[s]], out=Ct[:, s, 0, :],
                 in0=Cm[:, 1, i, :], scalar=NKIM[:, i:i + 1], in1=ctmp[:, :], op0=ALU.mult, op1=ALU.add)
            P.op("dve", "tensor_scalar", reads=[Cmb, prb], writes=[ctmpb], out=ctmp[:, :], in0=Cm[:, 1, i, :],
                 scalar1=KRE[:, i:i + 1], scalar2=None, op0=ALU.mult)
            P.op("dve", "scalar_tensor_tensor", reads=[Cmb, prb, ctmpb], writes=[Ctb[s]], out=Ct[:, s, 1, :],
                 in0=Cm[:, 0, i, :], scalar=NKIM[:, i:i + 1], in1=ctmp[:, :], op0=ALU.mult, op1=ALU.subtract)
            P.op("dve", "tensor_copy", reads=[prb], writes=[Eb], out=Ec[:, 0:1], in_=COS[:, i:i + 1])
            P.op("dve", "tensor_copy", reads=[prb], writes=[Eb], out=Es[:, 0:1], in_=SIN[:, i:i + 1])
            n = 1
            while n < S:
                cn, sn = Ec[:, n - 1:n], Es[:, n - 1:n]
                P.op("dve", "tensor_scalar", reads=[Eb], writes=[tmpAb], out=tmpA[:, 0:n], in0=Es[:, 0:n], scalar1=sn,
                     scalar2=None, op0=ALU.mult)
                P.op("dve", "tensor_scalar", reads=[Eb], writes=[tmpBb], out=tmpB[:, 0:n], in0=Ec[:, 0:n], scalar1=sn,
                     scalar2=None, op0=ALU.mult)
                P.op("dve", "scalar_tensor_tensor", reads=[Eb, tmpAb], writes=[Eb], out=Ec[:, n:2 * n], in0=Ec[:, 0:n],
                     scalar=cn, in1=tmpA[:, 0:n], op0=ALU.mult, op1=ALU.subtract)
                P.op("dve", "scalar_tensor_tensor", reads=[Eb, tmpBb], writes=[Eb], out=Es[:, n:2 * n], in0=Es[:, 0:n],
                     scalar=cn, in1=tmpB[:, 0:n], op0=ALU.mult, op1=ALU.add)
                n *= 2
            P.op("dve", "tensor_scalar", reads=[ones32b, prb], writes=[rhoTb], out=rhoT[:, :], in0=ones32[:, 0:T],
                 scalar1=RHO[:, i:i + 1], scalar2=None, op0=ALU.mult)
            for hh in range(2):
                t0 = hh * T
                for c in range(2):
                    P.op("pe", "matmul", reads=[BTb, ubb[s]], writes=[psb[c]], signal=False,
                         out=ps[c][:, :], lhsT=BT[:, 0, i, :], rhs=ub[:, s, t0 + c * 512:t0 + (c + 1) * 512],
                         start=True, stop=True)
                    P.op("pe", "matmul", reads=[BTb, ubb[s]], writes=[psb[2 + c]], signal=(c == 1),
                         out=ps[2 + c][:, :], lhsT=BT[:, 1, i, :], rhs=ub[:, s, t0 + c * 512:t0 + (c + 1) * 512],
                         start=True, stop=True)
                for c in range(2):
                    cs = slice(c * 512, (c + 1) * 512)
                    gs_ = slice(t0 + c * 512, t0 + (c + 1) * 512)
                    P.op("dve", "tensor_tensor", reads=[psb[c], Eb], writes=[tmpAb], out=tmpA[:, cs], in0=ps[c][:, :],
                         in1=Ec[:, gs_], op=ALU.mult)
                    P.op("dve", "tensor_tensor", reads=[psb[2 + c], Eb], writes=[tmpBb], out=tmpB[:, cs],
                         in0=ps[2 + c][:, :], in1=Es[:, gs_], op=ALU.mult)
                    P.op("dve", "tensor_tensor", reads=[tmpAb, tmpBb], writes=[Zrb[hh]], out=Zr[:, hh, cs],
                         in0=tmpA[:, cs], in1=tmpB[:, cs], op=ALU.add)
                    P.op("dve", "tensor_tensor", reads=[psb[2 + c], Eb], writes=[tmpAb], out=tmpA[:, cs],
                         in0=ps[2 + c][:, :], in1=Ec[:, gs_], op=ALU.mult)
                    P.op("dve", "tensor_tensor", reads=[psb[c], Eb], writes=[tmpBb], out=tmpB[:, cs], in0=ps[c][:, :],
                         in1=Es[:, gs_], op=ALU.mult)
                    P.op("dve", "tensor_tensor", reads=[tmpAb, tmpBb], writes=[Zib[hh]], out=Zi[:, hh, cs],
                         in0=tmpA[:, cs], in1=tmpB[:, cs], op=ALU.subtract)
                ir = 0.0 if hh == 0 else Zr[:, 0, T - 1:T]
                ii = 0.0 if hh == 0 else Zi[:, 0, T - 1:T]
                P.op("dve", "tensor_tensor_scan", reads=[rhoTb, Zrb[hh]] + ([Zrb[0]] if hh else []), writes=[Zrb[hh]],
                     out=Zr[:, hh, :], data0=rhoT[:, :], data1=Zr[:, hh, :], initial=ir, op0=ALU.mult, op1=ALU.add)
                P.op("dve", "tensor_tensor_scan", reads=[rhoTb, Zib[hh]] + ([Zib[0]] if hh else []), writes=[Zib[hh]],
                     out=Zi[:, hh, :], data0=rhoT[:, :], data1=Zi[:, hh, :], initial=ii, op0=ALU.mult, op1=ALU.add)
                gsl = slice(t0, t0 + T)
                P.op("dve", "tensor_tensor", reads=[Zrb[hh], Eb], writes=[tmpAb], out=tmpA[:, :], in0=Zr[:, hh, :],
                     in1=Ec[:, gsl], op=ALU.mult)
                P.op("dve", "tensor_tensor", reads=[Zib[hh], Eb], writes=[tmpBb], out=tmpB[:, :], in0=Zi[:, hh, :],
                     in1=Es[:, gsl], op=ALU.mult)
                P.op("dve", "tensor_tensor", reads=[tmpAb, tmpBb], writes=[xrb], out=xr[:, :], in0=tmpA[:, :],
                     in1=tmpB[:, :], op=ALU.subtract)
                P.op("dve", "tensor_tensor", reads=[Zib[hh], Eb], writes=[tmpAb], out=tmpA[:, :], in0=Zi[:, hh, :],
                     in1=Ec[:, gsl], op=ALU.mult)
                P.op("dve", "tensor_tensor", reads=[Zrb[hh], Eb], writes=[tmpBb], out=tmpB[:, :], in0=Zr[:, hh, :],
                     in1=Es[:, gsl], op=ALU.mult)
                P.op("dve", "tensor_tensor", reads=[tmpAb, tmpBb], writes=[xib], out=xi[:, :], in0=tmpA[:, :],
                     in1=tmpB[:, :], op=ALU.add)
                for c in range(2):
                    cs = slice(c * 512, (c + 1) * 512)
                    P.op("pe", "matmul", reads=[Ctb[s], xrb], writes=[ps6b], signal=False,
                         out=ps6[0:32, :], lhsT=Ct[:, s, 0, :], rhs=xr[:, cs], start=True, stop=False)
                    P.op("pe", "matmul", reads=[Ctb[s], xib], writes=[ps6b], signal=True,
                         out=ps6[0:32, :], lhsT=Ct[:, s, 1, :], rhs=xi[:, cs], start=False, stop=True)
                    gs_ = slice(t0 + c * 512, t0 + (c + 1) * 512)
                    P.op("dve", "scalar_tensor_tensor", reads=[utb[s], dskb, ps6b], writes=[yob[s]],
                         out=yo[:, s, gs_], in0=ut[:, s, gs_], scalar=dsk[:, i:i + 1], in1=ps6[0:32, :],
                         op0=ALU.mult, op1=ALU.add)
            P.dma("sp", yosem[s], out=yg[i * 32:(i + 1) * 32, :], in_=yo[:, s, :], reads=[yob[s]])

        qh = P.sb("qh", [128, 2, T], BF16); kh = P.sb("kh", [128, 2, S], BF16); vh = P.sb("vh", [128, 2, 16, 128], BF16)
        hb = [P.buf(), P.buf()]; hsem = [P.dsem("hs0"), P.dsem("hs1")]
        Bt = Ec; Lt = Es; Btb = Eb; Ltb = P.buf()
        Pt = P.sb("Pt", [128, S], F32); Ptb = P.buf()
        Wt = P.sb("Wt", [128, S], BF16); Wtb = P.buf()
        WT = P.sb("WT", [128, 16, 128], BF16); WTb = P.buf()
        ao = P.sb("ao", [128, 2, T], BF16); aob = [P.buf(), P.buf()]; aosem = [P.dsem("aos0"), P.dsem("aos1")]
        for h in range(16):
            s = h % 2
            P.dma("sp", hsem[s], out=qh[:, s, :], in_=qT[h * 128:(h + 1) * 128, :], writes=[hb[s]])
            P.dma("sp", hsem[s], out=kh[:, s, :], in_=kTf[h * 128:(h + 1) * 128, :], writes=[hb[s]])
            P.dma("sp", hsem[s], out=vh[:, s, :, :], in_=vhm[h], writes=[hb[s]])
            for i in range(8):
                nblk = 9 + i
                n = nblk * 128
                nch = (n + 511) // 512
                for c in range(nch):
                    w_ = min(512, n - c * 512)
                    P.op("pe", "matmul", reads=[hb[s]], writes=[psb[c]], signal=True,
                         out=ps[c][:, 0:w_], lhsT=qh[:, s, i * 128:(i + 1) * 128], rhs=kh[:, s, c * 512:c * 512 + w_],
                         start=True, stop=True)
                    P.op("act", "activation", reads=[psb[c]], writes=[Btb], out=Bt[:, c * 512:c * 512 + w_],
                         in_=ps[c][:, 0:w_], func=AF.Sigmoid)
                P.op("act", "activation", reads=[Btb], writes=[Ltb], out=Lt[:, 0:n], in_=Bt[:, 0:n], func=AF.Ln,
                     scale=-1.0, bias=ones32[:, 0:1])
                P.op("dve", "tensor_tensor", reads=[Ltb, maskb], writes=[Ltb], out=Lt[:, n - 128:n], in0=Lt[:, n - 128:n],
                     in1=mask[:, :], op=ALU.mult)
                P.op("dve", "tensor_tensor", reads=[Btb, maskb], writes=[Btb], out=Bt[:, n - 128:n], in0=Bt[:, n - 128:n],
                     in1=mask[:, :], op=ALU.mult)
                P.op("dve", "tensor_tensor_scan", reads=[ones32b, Ltb], writes=[Ptb], out=Pt[:, 0:n], data0=ones32[:, 0:n],
                     data1=Lt[:, 0:n], initial=0.0, op0=ALU.mult, op1=ALU.add)
                P.op("act", "activation", reads=[Ptb], writes=[Ltb], out=Lt[:, 0:n], in_=Pt[:, 0:n], func=AF.Exp,
                     scale=-1.0, bias=Pt[:, n - 1:n])
                P.op("dve", "tensor_tensor", reads=[Btb, Ltb], writes=[Wtb], out=Wt[:, 0:n], in0=Bt[:, 0:n], in1=Lt[:, 0:n],
                     op=ALU.mult)
                for blk in range(nblk):
                    tb = blk // 8
                    P.op("pe", "transpose", reads=[Wtb, identb], writes=[pTb[tb]], signal=(blk == nblk - 1 or blk == 7),
                         out=pT[tb][:, (blk % 8) * 128:(blk % 8 + 1) * 128], in_=Wt[:, blk * 128:(blk + 1) * 128],
                         identity=ident[:, :])
                P.op("act", "activation", reads=[pTb[0]], writes=[WTb], out=WT[:, 0:8, :],
                     in_=pT[0][:, :].rearrange("p (b q) -> p b q", b=8), func=AF.Copy)
                nb2 = nblk - 8
                P.op("dve", "tensor_copy", reads=[pTb[1]], writes=[WTb], out=WT[:, 8:8 + nb2, :],
                     in_=pT[1][:, 0:nb2 * 128].rearrange("p (b q) -> p b q", b=nb2))
                for blk in range(nblk):
                    P.op("pe", "matmul", reads=[hb[s], WTb], writes=[ps6b], signal=(blk == nblk - 1),
                         out=ps6[:, 0:128], lhsT=vh[:, s, blk + (7 - i), :] if False else vh[:, s, blk, :],
                         rhs=WT[:, blk, :], start=(blk == 0), stop=(blk == nblk - 1))
                P.op("act", "activation", reads=[ps6b], writes=[aob[s]], out=ao[:, s, i * 128:(i + 1) * 128],
                     in_=ps6[:, 0:128], func=AF.Copy)
            P.dma("sp", aosem[s], out=attnT[h * 128:(h + 1) * 128, :], in_=ao[:, s, :], reads=[aob[s]])
        P.finalize()
    return nc


def ssm_params(lam_re, lam_im, log_dt, b_re, b_im, c_re, c_im, d_skip, half):
    g0 = 64 * half
    G = np.arange(g0, g0 + 64).reshape(32, 2)
    lam = np.zeros((128, 96), np.float32)
    lam[:, 0:32] = lam_re[G].transpose(1, 2, 0).reshape(128, 32)
    lam[:, 32:64] = lam_im[G].transpose(1, 2, 0).reshape(128, 32)
    lam[:, 64:96] = np.repeat(log_dt[G].T[:, None, :], 64, axis=1).reshape(128, 32)
    BT = np.zeros((32, 2, 32, 128), np.float32)
    Cm = np.zeros((128, 2, 32, 32), np.float32)
    for gl in range(2):
        for ri, (b, c) in enumerate(((b_re, c_re), (b_im, c_im))):
            BT[gl * 16:(gl + 1) * 16, ri, :, gl * 64:(gl + 1) * 64] = b[G[:, gl]].transpose(2, 0, 1)
            Cm[gl * 64:(gl + 1) * 64, ri, :, gl * 16:(gl + 1) * 16] = c[G[:, gl]].transpose(2, 0, 1)
    dsk = d_skip.reshape(128, 16)[G].transpose(1, 2, 0).reshape(32, 32)
    return {"lam": lam, "BT": BT, "Cm": Cm, "dsk": np.ascontiguousarray(dsk)}

def attn_consts():
    return {"mask": np.tril(np.ones((128, 128), np.float32), -1), "ident": np.eye(128, dtype=np.float32).astype(bf)}

def v_headmajor(vfull):
    return np.ascontiguousarray(vfull.reshape(16, 128, 16, 128).transpose(2, 1, 0, 3))


NCORES = 8


def _lay(g):
    return np.ascontiguousarray(np.asarray(g, np.float32).reshape(32, 128).T)


def _run(nc, in_maps):
    res = run_bass_kernel_spmd(nc, in_maps, core_ids=list(range(NCORES)))
    return res.results


def kernel(x, p, g_mix, w_in, w_br_attn, lam_re, lam_im, log_dt, b_re, b_im, c_re, c_im, d_skip, w_glu,
           w_br_ssm, w_o, g_mlp, w_ff1, w_ff2, g_ple, w_ple_gate, w_ple, g_final):
    A = lambda a: np.asarray(a, dtype=np.float32)
    x, p = A(x), A(p)
    depth = x.shape[0] and A(g_mix).shape[0]
    cores = [(c // 2, c % 2) for c in range(NCORES)]
    hT = [np.ascontiguousarray(x[b, hf * T:(hf + 1) * T, :].T) for (b, hf) in cores]
    consts = attn_consts()
    for l in range(depth):
        nc1 = build_L1()
        wl = np.ascontiguousarray(A(w_in[l]))
        gm = _lay(g_mix[l])
        r1 = _run(nc1, [{"hT": hT[c], "gmix": gm, "w_in": wl} for c in range(NCORES)])
        del wl
        in2 = []
        for c, (b, hf) in enumerate(cores):
            c0, c1 = 2 * b, 2 * b + 1
            if hf == 1:
                kTf = np.concatenate([r1[c0]["kT"], r1[c1]["kT"]], axis=1)
                vfull = np.concatenate([r1[c0]["v"], r1[c1]["v"]], axis=0)
            else:
                kTf = np.concatenate([np.zeros_like(r1[c0]["kT"]), r1[c0]["kT"]], axis=1)
                vfull = np.concatenate([np.zeros_like(r1[c0]["v"]), r1[c0]["v"]], axis=0)
            ufull = np.concatenate([r1[c0]["uT"], r1[c1]["uT"]], axis=1)
            d = {"qT": r1[c]["qT"], "kTf": np.ascontiguousarray(kTf), "vhm": v_headmajor(vfull),
                 "uTg": np.ascontiguousarray(ufull[hf * 1024:(hf + 1) * 1024, :])}
            d.update(consts)
            d.update(ssm_params(A(lam_re[l]), A(lam_im[l]), A(log_dt[l]), A(b_re[l]), A(b_im[l]), A(c_re[l]),
                                A(c_im[l]), A(d_skip[l]), hf))
            in2.append(d)
        nc2 = build_L2()
        r2 = _run(nc2, in2)
        del in2
        final = (l == depth - 1)
        nc3 = build_L3(final)
        gvec = np.concatenate([_lay(g_mlp[l]), _lay(g_ple[l]), _lay(g_final)], axis=1)
        W = {"w_bra": A(w_br_attn[l]), "w_glu": A(w_glu[l]), "w_brs": A(w_br_ssm[l]), "w_o": A(w_o[l]),
             "w_ff1": A(w_ff1[l]), "w_ff2": A(w_ff2[l]), "w_pg": A(w_ple_gate[l]), "w_ple": A(w_ple[l])}
        W = {k_: np.ascontiguousarray(v_) for k_, v_ in W.items()}
        in3 = []
        for c, (b, hf) in enumerate(cores):
            c0, c1 = 2 * b, 2 * b + 1
            yfull = np.concatenate([r2[c0]["yg"], r2[c1]["yg"]], axis=0)
            d = {"attnT": r2[c]["attnT"], "yT": np.ascontiguousarray(yfull[:, hf * T:(hf + 1) * T]),
                 "gT": r1[c]["gT"], "hT": hT[c], "pT": np.ascontiguousarray(p[l, b, hf * T:(hf + 1) * T, :].T),
                 "gvec": gvec}
            d.update(W)
            in3.append(d)
        r3 = _run(nc3, in3)
        del in3, W, r1, r2
        hT = [r3[c]["hout"] for c in range(NCORES)]
    out = np.zeros(x.shape, np.float32)
    for c, (b, hf) in enumerate(cores):
        out[b, hf * T:(hf + 1) * T, :] = hT[c].T
    return out
```

```python
import ml_dtypes
from concourse.bass_utils import run_bass_kernel_spmd
bf = ml_dtypes.bfloat16
import contextlib
import numpy as np
import concourse.bass as bass
import concourse.mybir as mybir

F32 = mybir.dt.float32
BF16 = mybir.dt.bfloat16
AF = mybir.ActivationFunctionType
ALU = mybir.AluOpType

ENG = {"pe": "tensor", "act": "scalar", "dve": "vector", "pool": "gpsimd", "sp": "sync"}


class Buf:
    __slots__ = ("name", "w", "r")

    def __init__(self, name):
        self.name = name
        self.w = None
        self.r = {}


class DSem:
    def __init__(self, handle):
        self.handle = handle
        self.count = 0


class Prog:
    def __init__(self, nc, stack):
        self.nc = nc
        self.stack = stack
        self.q = {e: [] for e in ENG}
        self.seq = {e: 0 for e in ENG}
        self.psem = {e: stack.enter_context(nc.semaphore("p_" + e)) for e in ENG if e != "sp"}
        self.known = {e: {} for e in ENG}
        self.dsems = []
        self.nbuf = 0

    def sb(self, name, shape, dt):
        return self.stack.enter_context(self.nc.sbuf_tensor(name, shape, dt))

    def psum(self, name, shape, dt):
        return self.stack.enter_context(self.nc.psum_tensor(name, shape, dt))

    def dsem(self, name):
        d = DSem(self.stack.enter_context(self.nc.semaphore(name)))
        self.dsems.append(d)
        return d

    def buf(self, name=None):
        self.nbuf += 1
        return Buf(name or "b%d" % self.nbuf)

    def _wait(self, eng, ev):
        if ev is None:
            return
        kind, key, val = ev
        if kind == "e":
            if key == eng and eng == "pe":
                return
            sem = self.psem[key]
            kid = ("e", key)
        else:
            sem = key.handle
            kid = ("d", id(key))
        if self.known[eng].get(kid, 0) >= val:
            return
        self.known[eng][kid] = val
        self.q[eng].append(("wait", sem, val))

    def _deps(self, eng, reads, writes):
        for b in reads:
            self._wait(eng, b.w)
        for b in writes:
            self._wait(eng, b.w)
            for kid, ev in b.r.items():
                self._wait(eng, ev)

    @staticmethod
    def _addr(b, ev):
        kid = (ev[0], ev[1] if ev[0] == "e" else id(ev[1]))
        old = b.r.get(kid)
        if old is None or old[2] < ev[2]:
            b.r[kid] = ev

    def op(self, eng, method, reads=(), writes=(), signal=True, **kw):
        self._deps(eng, reads, writes)
        ev = ("e", eng, self.seq[eng] + 1)
        if signal:
            self.seq[eng] += 1
        self.q[eng].append(("ins", method, kw, self.psem[eng] if signal else None, 1))
        for b in reads:
            self._addr(b, ev)
        for b in writes:
            b.w = ev
            b.r = {}

    def dma(self, eng, sem, out, in_, reads=(), writes=()):
        self._deps(eng, reads, writes)
        sem.count += 16
        ev = ("d", sem, sem.count)
        self.q[eng].append(("ins", "dma_start", dict(out=out, in_=in_), sem.handle, 16))
        for b in reads:
            self._addr(b, ev)
        for b in writes:
            b.w = ev
            b.r = {}

    def finalize(self):
        for d in self.dsems:
            if d.count:
                self._wait("sp", ("d", d, d.count))
        for e in ENG:
            if e != "sp" and self.seq[e]:
                self._wait("sp", ("e", e, self.seq[e]))
        nc = self.nc
        q = self.q

        def run(engobj, items):
            for it in items:
                if it[0] == "wait":
                    engobj.wait_ge(it[1], it[2])
                else:
                    _, method, kw, sem, inc = it
                    r = getattr(engobj, method)(**kw)
                    if sem is not None:
                        r.then_inc(sem, inc)

        with nc.Block() as block:
            @block.tensor
            def _(e):
                run(e, q["pe"])

            @block.scalar
            def _(e):
                run(e, q["act"])

            @block.vector
            def _(e):
                run(e, q["dve"])

            @block.gpsimd
            def _(e):
                run(e, q["pool"])

            @block.sync
            def _(e):
                run(e, q["sp"])


D = 4096
T = 1024
S = 2048
KC = D // 128
EPS = 1e-6
WSLOT = 8192
NWS = 3


class Ctx:
    def __init__(self, nc, stack):
        self.nc = nc
        self.P = P = Prog(nc, stack)
        self.ps = [P.psum("ps%d" % i, [128, 512], F32) for i in range(8)]
        self.psb = [P.buf("psb%d" % i) for i in range(8)]
        self.ws = [P.sb("ws%d" % i, [128, WSLOT], BF16) for i in range(NWS)]
        self.wsb = [P.buf("wsb%d" % i) for i in range(NWS)]
        self.wsem = [P.dsem("wsem%d" % i) for i in range(NWS)]
        self.wi = 0
        self.ones = P.sb("ones", [128, 128], BF16)
        self.onesb = P.buf("ones")
        P.op("dve", "memset", writes=[self.onesb], ap=self.ones[:], constant=1.0)
        self.onesf = P.sb("onesf", [128, 8], F32)
        self.onesfb = P.buf("onesf")
        P.op("dve", "memset", writes=[self.onesfb], ap=self.onesf[:], constant=1.0)
        self.epsc = P.sb("epsc", [128, 1], F32)
        self.epsb = P.buf("epsc")
        P.op("dve", "memset", writes=[self.epsb], ap=self.epsc[:], constant=EPS)
        self.bank_rr = 0

    def next_ws(self):
        i = self.wi % NWS
        self.wi += 1
        return i


def load_w(cx, W, r0, kc, c0, wc):
    P = cx.P
    i = cx.next_ws()
    view = cx.ws[i][:, 0:kc * wc].rearrange("p (k n) -> p k n", k=kc)
    src = W[r0:r0 + kc * 128, c0:c0 + wc].rearrange("(k p) n -> p k n", p=128)
    P.dma("pool", cx.wsem[i], out=view, in_=src, writes=[cx.wsb[i]])
    return view, cx.wsb[i]


def gemm_fm(cx, XT, XTb, kc, W, col0, ncols, epi, banks, r0=0, tk=T):
    P = cx.P
    wc = min(512, WSLOT // kc, ncols)
    nh = tk // 512
    jidx = 0
    bi = 0
    for c0 in range(col0, col0 + ncols, wc):
        wv, wb = load_w(cx, W, r0, kc, c0, wc)
        for jj in range(wc // 128):
            bset = banks[bi % len(banks)]
            bi += 1
            for h in range(nh):
                b = bset[h]
                for k in range(kc):
                    last = (k == kc - 1)
                    P.op("pe", "matmul", reads=[wb, XTb], writes=[cx.psb[b]],
                         signal=(last and h == nh - 1),
                         out=cx.ps[b][:, :], lhsT=wv[:, k, jj * 128:(jj + 1) * 128],
                         rhs=XT[:, k, h * 512:(h + 1) * 512], start=(k == 0), stop=last)
            epi(jidx, bset)
            jidx += 1


def prep_norm(cx, hT_dram, hT_b, g_sb, XT, XTb, st):
    P = cx.P
    ssq_banks = (6, 7)
    for fc in range(KC):
        s = fc % 2
        P.dma("sp", st["hsem"][s], out=st["hst"][:, s, :], in_=hT_dram[fc * 128:(fc + 1) * 128, :],
              reads=[hT_b], writes=[st["hstb"][s]])
        P.op("act", "activation", reads=[st["hstb"][s]], writes=[st["sqb"][s]],
             out=st["sq"][:, s, :], in_=st["hst"][:, s, :], func=AF.Square)
        P.op("dve", "tensor_scalar", reads=[st["hstb"][s], st["gb"]], writes=[XTb],
             out=XT[:, fc, :], in0=st["hst"][:, s, :], scalar1=g_sb[:, fc:fc + 1], scalar2=None,
             op0=ALU.mult)
        for h in range(2):
            P.op("pe", "matmul", reads=[st["sqb"][s], cx.onesb], writes=[cx.psb[ssq_banks[h]]],
                 signal=(h == 1),
                 out=cx.ps[ssq_banks[h]][:, :], lhsT=cx.ones[:, :], rhs=st["sq"][:, s, h * 512:(h + 1) * 512],
                 start=(fc == 0), stop=(fc == KC - 1))
    for h in range(2):
        P.op("act", "activation", reads=[cx.psb[ssq_banks[h]], cx.epsb], writes=[st["rstdb"]],
             out=st["rstd"][:, h * 512:(h + 1) * 512], in_=cx.ps[ssq_banks[h]][:, :], func=AF.Sqrt,
             scale=1.0 / D, bias=cx.epsc[:, 0:1])
    P.op("dve", "reciprocal", reads=[st["rstdb"]], writes=[st["rstdb"]],
         out=st["rstd"][:, :], in_=st["rstd"][:, :])


def alloc_norm_state(cx, name):
    P = cx.P
    st = {}
    st["hst"] = P.sb(name + "_hst", [128, 2, T], F32)
    st["hstb"] = [P.buf(), P.buf()]
    st["hsem"] = [P.dsem(name + "_hs0"), P.dsem(name + "_hs1")]
    st["sq"] = P.sb(name + "_sq", [128, 2, T], BF16)
    st["sqb"] = [P.buf(), P.buf()]
    st["rstd"] = P.sb(name + "_rstd", [128, T], F32)
    st["rstdb"] = P.buf()
    return st


def build_L1():
    nc = bass.Bass("TRN2", target_bir_lowering=False)
    hT = nc.dram_tensor("hT", [D, T], F32, kind="ExternalInput").ap()
    gmix = nc.dram_tensor("gmix", [128, KC], F32, kind="ExternalInput").ap()
    w_in = nc.dram_tensor("w_in", [D, 16384], F32, kind="ExternalInput").ap()
    qT = nc.dram_tensor("qT", [2048, T], BF16, kind="ExternalOutput").ap()
    kT = nc.dram_tensor("kT", [2048, T], BF16, kind="ExternalOutput").ap()
    vv = nc.dram_tensor("v", [T, 2048], BF16, kind="ExternalOutput").ap()
    uT = nc.dram_tensor("uT", [2048, T], F32, kind="ExternalOutput").ap()
    gT = nc.dram_tensor("gT", [8192, T], BF16, kind="ExternalOutput").ap()
    with contextlib.ExitStack() as stack:
        cx = Ctx(nc, stack)
        P = cx.P
        hTb = P.buf("hT")
        XT = P.sb("XT", [128, KC, T], BF16)
        XTb = P.buf("XT")
        st = alloc_norm_state(cx, "n1")
        g_sb = P.sb("g_sb", [128, KC], F32)
        st["gb"] = P.buf("g")
        gsem = P.dsem("gsem")
        P.dma("sp", gsem, out=g_sb[:, :], in_=gmix[:, :], writes=[st["gb"]])
        prep_norm(cx, hT, hTb, g_sb, XT, XTb, st)
        rstd, rstdb = st["rstd"], st["rstdb"]
        rtok = P.sb("rtok", [128, 8], F32)
        rtokb = P.buf("rtok")
        for tt in range(8):
            P.op("pe", "matmul", reads=[rstdb, cx.onesfb], writes=[cx.psb[5]], signal=(tt == 7),
                 out=cx.ps[5][:, tt:tt + 1], lhsT=rstd[0:1, tt * 128:(tt + 1) * 128], rhs=cx.onesf[0:1, 0:1],
                 start=True, stop=True)
        P.op("dve", "tensor_copy", reads=[cx.psb[5]], writes=[rtokb], out=rtok[:, :], in_=cx.ps[5][:, 0:8])

        ob16 = P.sb("ob16", [128, 3, T], BF16)
        ob16b = [P.buf() for _ in range(3)]
        ob32 = P.sb("ob32", [128, 2, T], F32)
        ob32b = [P.buf() for _ in range(2)]
        osem = [P.dsem("osem%d" % i) for i in range(3)]
        osem32 = [P.dsem("osem32_%d" % i) for i in range(2)]
        outb = P.buf("outs")
        cnt = {"o16": 0, "o32": 0}
        banks = [(0, 1), (2, 3)]
        SCALE = 128.0 ** -0.5

        def epi_qk(dst, scale):
            def epi(j, bset):
                s = cnt["o16"] % 3
                cnt["o16"] += 1
                for h in range(2):
                    if scale is not None:
                        P.op("dve", "scalar_tensor_tensor", reads=[cx.psb[bset[h]], rstdb], writes=[ob16b[s]],
                             out=ob16[:, s, h * 512:(h + 1) * 512], in0=cx.ps[bset[h]][:, :], scalar=scale,
                             in1=rstd[:, h * 512:(h + 1) * 512], op0=ALU.mult, op1=ALU.mult)
                    else:
                        P.op("dve", "tensor_tensor", reads=[cx.psb[bset[h]], rstdb], writes=[ob16b[s]],
                             out=ob16[:, s, h * 512:(h + 1) * 512], in0=cx.ps[bset[h]][:, :],
                             in1=rstd[:, h * 512:(h + 1) * 512], op=ALU.mult)
                P.dma("sp", osem[s], out=dst[j * 128:(j + 1) * 128, :], in_=ob16[:, s, :], reads=[ob16b[s]],
                      writes=[])
            return epi

        def epi_u(j, bset):
            s = cnt["o32"] % 2
            cnt["o32"] += 1
            for h in range(2):
                P.op("dve", "tensor_tensor", reads=[cx.psb[bset[h]], rstdb], writes=[ob32b[s]],
                     out=ob32[:, s, h * 512:(h + 1) * 512], in0=cx.ps[bset[h]][:, :],
                     in1=rstd[:, h * 512:(h + 1) * 512], op=ALU.mult)
            P.dma("sp", osem32[s], out=uT[j * 128:(j + 1) * 128, :], in_=ob32[:, s, :], reads=[ob32b[s]])

        def epi_g(j, bset):
            s32 = cnt["o32"] % 2
            cnt["o32"] += 1
            s = cnt["o16"] % 3
            cnt["o16"] += 1
            for h in range(2):
                P.op("dve", "tensor_tensor", reads=[cx.psb[bset[h]], rstdb], writes=[ob32b[s32]],
                     out=ob32[:, s32, h * 512:(h + 1) * 512], in0=cx.ps[bset[h]][:, :],
                     in1=rstd[:, h * 512:(h + 1) * 512], op=ALU.mult)
            P.op("act", "activation", reads=[ob32b[s32]], writes=[ob16b[s]],
                 out=ob16[:, s, :], in_=ob32[:, s32, :], func=AF.Sigmoid)
            P.dma("sp", osem[s], out=gT[j * 128:(j + 1) * 128, :], in_=ob16[:, s, :], reads=[ob16b[s]])

        gemm_fm(cx, XT, XTb, KC, w_in, 0, 2048, epi_qk(qT, SCALE), banks)
        gemm_fm(cx, XT, XTb, KC, w_in, 2048, 2048, epi_qk(kT, None), banks)
        wc = WSLOT // KC
        vb16 = P.sb("vb16", [128, 2, 256], BF16)
        vbb = [P.buf(), P.buf()]
        vsem = [P.dsem("vsem0"), P.dsem("vsem1")]
        vi = 0
        for c0 in range(0, 2048, wc):
            wv, wb = load_w(cx, w_in, 0, KC, 4096 + c0, wc)
            for tt in range(8):
                b = (0, 1, 2, 3)[vi % 4]
                for k in range(KC):
                    P.op("pe", "matmul", reads=[wb, XTb], writes=[cx.psb[b]], signal=(k == KC - 1),
                         out=cx.ps[b][:, 0:wc], lhsT=XT[:, k, tt * 128:(tt + 1) * 128], rhs=wv[:, k, :],
                         start=(k == 0), stop=(k == KC - 1))
                s = vi % 2
                vi += 1
                P.op("dve", "tensor_scalar", reads=[cx.psb[b], rtokb], writes=[vbb[s]],
                     out=vb16[:, s, :], in0=cx.ps[b][:, 0:wc], scalar1=rtok[:, tt:tt + 1], scalar2=None,
                     op0=ALU.mult)
                P.dma("sp", vsem[s], out=vv[tt * 128:(tt + 1) * 128, c0:c0 + wc], in_=vb16[:, s, :],
                      reads=[vbb[s]])
        gemm_fm(cx, XT, XTb, KC, w_in, 6144, 2048, epi_u, banks)
        gemm_fm(cx, XT, XTb, KC, w_in, 8192, 8192, epi_g, banks)
        P.finalize()
    return nc


def build_L3(final):
    nc = bass.Bass("TRN2", target_bir_lowering=False)
    dt_ = nc.dram_tensor
    attnT = dt_("attnT", [2048, T], BF16, kind="ExternalInput").ap()
    yT = dt_("yT", [2048, T], F32, kind="ExternalInput").ap()
    gT = dt_("gT", [8192, T], BF16, kind="ExternalInput").ap()
    hT = dt_("hT", [D, T], F32, kind="ExternalInput").ap()
    pT = dt_("pT", [256, T], F32, kind="ExternalInput").ap()
    w_bra = dt_("w_bra", [2048, D], F32, kind="ExternalInput").ap()
    w_glu = dt_("w_glu", [2048, 2048], F32, kind="ExternalInput").ap()
    w_brs = dt_("w_brs", [2048, D], F32, kind="ExternalInput").ap()
    w_o = dt_("w_o", [D, D], F32, kind="ExternalInput").ap()
    w_ff1 = dt_("w_ff1", [D, 16384], F32, kind="ExternalInput").ap()
    w_ff2 = dt_("w_ff2", [16384, D], F32, kind="ExternalInput").ap()
    w_pg = dt_("w_pg", [D, D], F32, kind="ExternalInput").ap()
    w_ple = dt_("w_ple", [256, D], F32, kind="ExternalInput").ap()
    gvec = dt_("gvec", [128, 3 * KC], F32, kind="ExternalInput").ap()
    hout = dt_("hout", [D, T], F32, kind="ExternalOutput").ap()
    h2 = dt_("h2", [D, T], F32).ap()
    h3 = dt_("h3", [D, T], F32).ap()
    h4 = dt_("h4", [D, T], F32).ap()
    aT = dt_("aT", [16384, T], BF16).ap()
    with contextlib.ExitStack() as stack:
        cx = Ctx(nc, stack)
        P = cx.P
        R1 = P.sb("R1", [128, KC, T], BF16)
        R2 = P.sb("R2", [128, KC, T], BF16)
        R1a, R1b, R2b = P.buf("R1a"), P.buf("R1b"), P.buf("R2")
        g_sb = P.sb("g_sb", [128, 3 * KC], F32)
        gb = P.buf("g")
        msem = P.dsem("msem")
        P.dma("sp", msem, out=g_sb[:, :], in_=gvec[:, :], writes=[gb])
        st32 = P.sb("st32", [128, 3, T], F32)
        st32b = [P.buf() for _ in range(3)]
        st32s = [P.dsem("st32s%d" % i) for i in range(3)]
        st16 = P.sb("st16", [128, 3, T], BF16)
        st16b = [P.buf() for _ in range(3)]
        st16s = [P.dsem("st16s%d" % i) for i in range(3)]
        rstd = P.sb("rstd", [128, T], F32)
        rstdb = P.buf("rstd")
        ssqa = P.sb("ssqa", [128, T], F32)
        ssqab = P.buf("ssqa")
        c32 = [0]
        c16 = [0]

        def n32():
            c32[0] += 1
            return (c32[0] - 1) % 3

        def n16():
            c16[0] += 1
            return (c16[0] - 1) % 3

        banks = [(0, 1), (2, 3)]
        h2b = [P.buf() for _ in range(KC)]
        h3b = [P.buf() for _ in range(KC)]
        h4b = [P.buf() for _ in range(KC)]
        aTb = [P.buf() for _ in range(128)]

        for c in range(16):
            s = n32()
            s2 = n32()
            P.dma("sp", st32s[s], out=st32[:, s, :], in_=yT[c * 128:(c + 1) * 128, :], writes=[st32b[s]])
            P.op("act", "activation", reads=[st32b[s]], writes=[st32b[s2]],
                 out=st32[:, s2, :], in_=st32[:, s, :], func=AF.Square)
            P.op("dve", "tensor_scalar", reads=[st32b[s2]], writes=[st32b[s2]],
                 out=st32[:, s2, :], in0=st32[:, s2, :], scalar1=0.044715, scalar2=1.0, op0=ALU.mult, op1=ALU.add)
            P.op("dve", "tensor_tensor", reads=[st32b[s2], st32b[s]], writes=[st32b[s2]],
                 out=st32[:, s2, :], in0=st32[:, s2, :], in1=st32[:, s, :], op=ALU.mult)
            P.op("act", "activation", reads=[st32b[s2]], writes=[st32b[s2]],
                 out=st32[:, s2, :], in_=st32[:, s2, :], func=AF.Sigmoid, scale=1.5957691216057308)
            P.op("dve", "tensor_tensor", reads=[st32b[s2], st32b[s]], writes=[R1a],
                 out=R1[:, c, :], in0=st32[:, s2, :], in1=st32[:, s, :], op=ALU.mult)

        def epi_glu(j, bset):
            s = n32()
            for h in range(2):
                P.op("act", "activation", reads=[cx.psb[bset[h]]], writes=[st32b[s]],
                     out=st32[:, s, h * 512:(h + 1) * 512], in_=cx.ps[bset[h]][:, :], func=AF.Sigmoid)
            P.op("dve", "tensor_tensor", reads=[st32b[s], R1a], writes=[R1b],
                 out=R1[:, 16 + j, :], in0=st32[:, s, :], in1=R1[:, j, :], op=ALU.mult)

        gemm_fm(cx, R1[:, 0:16, :], R1a, 16, w_glu, 0, 2048, epi_glu, banks)

        atsem = P.dsem("atsem")
        for c in range(16):
            P.dma("sp", atsem, out=R1[:, c, :], in_=attnT[c * 128:(c + 1) * 128, :], writes=[R1a])

        for j0 in range(0, KC, 4):
            wa, wab = load_w(cx, w_bra, 0, 16, j0 * 128, 512)
            wsv, wsb_ = load_w(cx, w_brs, 0, 16, j0 * 128, 512)
            for jj in range(4):
                j = j0 + jj
                for (wv, wb, xoff, xb, bset) in ((wa, wab, 0, R1a, (0, 1)), (wsv, wsb_, 16, R1b, (2, 3))):
                    for h in range(2):
                        for k in range(16):
                            P.op("pe", "matmul", reads=[wb, xb], writes=[cx.psb[bset[h]]],
                                 signal=(k == 15 and h == 1),
                                 out=cx.ps[bset[h]][:, :], lhsT=wv[:, k, jj * 128:(jj + 1) * 128],
                                 rhs=R1[:, xoff + k, h * 512:(h + 1) * 512], start=(k == 0), stop=(k == 15))
                sa, sg = n16(), n16()
                P.dma("sp", st16s[sa], out=st16[:, sa, :], in_=gT[j * 128:(j + 1) * 128, :], writes=[st16b[sa]])
                P.dma("sp", st16s[sg], out=st16[:, sg, :], in_=gT[(32 + j) * 128:(33 + j) * 128, :],
                      writes=[st16b[sg]])
                s1, s2 = n32(), n32()
                for h in range(2):
                    hs = slice(h * 512, (h + 1) * 512)
                    P.op("dve", "tensor_tensor", reads=[cx.psb[0 + h], st16b[sa]], writes=[st32b[s1]],
                         out=st32[:, s1, hs], in0=cx.ps[0 + h][:, :], in1=st16[:, sa, hs], op=ALU.mult)
                    P.op("dve", "tensor_tensor", reads=[cx.psb[2 + h], st16b[sg]], writes=[st32b[s2]],
                         out=st32[:, s2, hs], in0=cx.ps[2 + h][:, :], in1=st16[:, sg, hs], op=ALU.mult)
                P.op("dve", "tensor_tensor", reads=[st32b[s1], st32b[s2]], writes=[R2b],
                     out=R2[:, j, :], in0=st32[:, s1, :], in1=st32[:, s2, :], op=ALU.add)

        sq = P.sb("sq", [128, 2, T], BF16)
        sqb = [P.buf(), P.buf()]
        sqc = [0]

        def make_rstd(src_is_psum, ssq_banks=None):
            for h in range(2):
                hs = slice(h * 512, (h + 1) * 512)
                if src_is_psum:
                    P.op("act", "activation", reads=[cx.psb[ssq_banks[h]], cx.epsb], writes=[rstdb],
                         out=rstd[:, hs], in_=cx.ps[ssq_banks[h]][:, :], func=AF.Sqrt, scale=1.0 / D,
                         bias=cx.epsc[:, 0:1])
                else:
                    P.op("act", "activation", reads=[ssqab, cx.epsb], writes=[rstdb],
                         out=rstd[:, hs], in_=ssqa[:, hs], func=AF.Sqrt, scale=1.0 / D, bias=cx.epsc[:, 0:1])
            P.op("dve", "reciprocal", reads=[rstdb], writes=[rstdb], out=rstd[:, :], in_=rstd[:, :])

        def epi_wo(j, bset):
            s = n32()
            P.dma("sp", st32s[s], out=st32[:, s, :], in_=hT[j * 128:(j + 1) * 128, :], writes=[st32b[s]])
            for h in range(2):
                hs = slice(h * 512, (h + 1) * 512)
                P.op("dve", "tensor_tensor", reads=[cx.psb[bset[h]], st32b[s]], writes=[st32b[s]],
                     out=st32[:, s, hs], in0=cx.ps[bset[h]][:, :], in1=st32[:, s, hs], op=ALU.add)
            P.dma("sp", st32s[s], out=h2[j * 128:(j + 1) * 128, :], in_=st32[:, s, :],
                  reads=[st32b[s]], writes=[h2b[j]])
            q = sqc[0] % 2
            sqc[0] += 1
            P.op("act", "activation", reads=[st32b[s]], writes=[sqb[q]],
                 out=sq[:, q, :], in_=st32[:, s, :], func=AF.Square)
            P.op("dve", "tensor_scalar", reads=[st32b[s], gb], writes=[R1a, R1b],
                 out=R1[:, j, :], in0=st32[:, s, :], scalar1=g_sb[:, j:j + 1], scalar2=None, op0=ALU.mult)
            for h in range(2):
                P.op("pe", "matmul", reads=[sqb[q], cx.onesb], writes=[cx.psb[6 + h]],
                     signal=(h == 1), out=cx.ps[6 + h][:, :], lhsT=cx.ones[:, :],
                     rhs=sq[:, q, h * 512:(h + 1) * 512], start=(j == 0), stop=(j == KC - 1))

        gemm_fm(cx, R2, R2b, KC, w_o, 0, D, epi_wo, banks)
        make_rstd(True, (6, 7))

        def epi_ff1(f, bset):
            s = n32()
            for h in range(2):
                hs = slice(h * 512, (h + 1) * 512)
                P.op("dve", "scalar_tensor_tensor", reads=[cx.psb[bset[h]], rstdb], writes=[st32b[s]],
                     out=st32[:, s, hs], in0=cx.ps[bset[h]][:, :], scalar=0.0, in1=rstd[:, hs],
                     op0=ALU.max, op1=ALU.mult)
            s16 = n16()
            P.op("act", "activation", reads=[st32b[s]], writes=[st16b[s16]],
                 out=st16[:, s16, :], in_=st32[:, s, :], func=AF.Square)
            P.dma("sp", st16s[s16], out=aT[f * 128:(f + 1) * 128, :], in_=st16[:, s16, :],
                  reads=[st16b[s16]], writes=[aTb[f]])

        def gemm_fm2(XT, xbufs, kc, W, col0, ncols, epi):
            wc = min(512, WSLOT // kc, ncols)
            jidx = 0
            bi = 0
            for c0 in range(col0, col0 + ncols, wc):
                wv, wb = load_w(cx, W, 0, kc, c0, wc)
                for jj in range(wc // 128):
                    bset = banks[bi % 2]
                    bi += 1
                    for h in range(2):
                        for k in range(kc):
                            last = (k == kc - 1)
                            P.op("pe", "matmul", reads=[wb] + xbufs, writes=[cx.psb[bset[h]]],
                                 signal=(last and h == 1), out=cx.ps[bset[h]][:, :],
                                 lhsT=wv[:, k, jj * 128:(jj + 1) * 128], rhs=XT[:, k, h * 512:(h + 1) * 512],
                                 start=(k == 0), stop=last)
                    epi(jidx, bset)
                    jidx += 1

        gemm_fm2(R1, [R1a, R1b], KC, w_ff1, 0, 16384, epi_ff1)

        aslot_b = [P.buf() for _ in range(4)]
        aslot_s = [P.dsem("asl%d" % i) for i in range(4)]
        ai = [0]
        first_ssq = [True]
        for ps_ in range(8):
            for fg in range(8):
                wv, wb = load_w(cx, w_ff2, fg * 2048, 16, ps_ * 512, 512)
                for f4 in range(4):
                    sl = ai[0] % 4
                    ai[0] += 1
                    f0 = fg * 16 + f4 * 4
                    extra = [R1a, R1b] if (ps_ == 0 and fg == 0 and f4 < 4 and ai[0] <= 4) else []
                    P.dma("sp", aslot_s[sl], out=R1[:, sl * 4:(sl + 1) * 4, :],
                          in_=aT[f0 * 128:(f0 + 4) * 128, :].rearrange("(c p) t -> p c t", p=128),
                          reads=[aTb[f0 + i] for i in range(4)], writes=[aslot_b[sl]] + extra)
                    for fi in range(4):
                        f = f0 + fi
                        kf = f4 * 4 + fi
                        for jj in range(4):
                            for h in range(2):
                                b = jj * 2 + h
                                lastf = (f == 127)
                                P.op("pe", "matmul", reads=[wb, aslot_b[sl]], writes=[cx.psb[b]],
                                     signal=(jj == 3 and h == 1 and (fi == 3 or lastf)),
                                     out=cx.ps[b][:, :], lhsT=wv[:, kf, jj * 128:(jj + 1) * 128],
                                     rhs=R1[:, sl * 4 + fi, h * 512:(h + 1) * 512], start=(f == 0), stop=lastf)
            qs = []
            for jj in range(4):
                j = ps_ * 4 + jj
                s = n32()
                P.dma("sp", st32s[s], out=st32[:, s, :], in_=h2[j * 128:(j + 1) * 128, :],
                      reads=[h2b[j]], writes=[st32b[s]])
                for h in range(2):
                    hs = slice(h * 512, (h + 1) * 512)
                    P.op("dve", "tensor_tensor", reads=[cx.psb[jj * 2 + h], st32b[s]], writes=[st32b[s]],
                         out=st32[:, s, hs], in0=cx.ps[jj * 2 + h][:, :], in1=st32[:, s, hs], op=ALU.add)
                P.dma("sp", st32s[s], out=h3[j * 128:(j + 1) * 128, :], in_=st32[:, s, :],
                      reads=[st32b[s]], writes=[h3b[j]])
                P.op("dve", "tensor_scalar", reads=[st32b[s], gb], writes=[R2b],
                     out=R2[:, j, :], in0=st32[:, s, :], scalar1=g_sb[:, KC + j:KC + j + 1], scalar2=None,
                     op0=ALU.mult)
                q = sqc[0] % 2
                sqc[0] += 1
                P.op("act", "activation", reads=[st32b[s]], writes=[sqb[q]],
                     out=sq[:, q, :], in_=st32[:, s, :], func=AF.Square)
                qs.append(q)
                for h in range(2):
                    P.op("pe", "matmul", reads=[sqb[q], cx.onesb], writes=[cx.psb[h]],
                         signal=(h == 1), out=cx.ps[h][:, :], lhsT=cx.ones[:, :],
                         rhs=sq[:, q, h * 512:(h + 1) * 512], start=(jj == 0), stop=(jj == 3))
            for h in range(2):
                hs = slice(h * 512, (h + 1) * 512)
                if ps_ == 0:
                    P.op("dve", "tensor_copy", reads=[cx.psb[h]], writes=[ssqab], out=ssqa[:, hs], in_=cx.ps[h][:, :])
                else:
                    P.op("dve", "tensor_tensor", reads=[cx.psb[h], ssqab], writes=[ssqab],
                         out=ssqa[:, hs], in0=cx.ps[h][:, :], in1=ssqa[:, hs], op=ALU.add)
        make_rstd(False)

        PT = st16[:, 0:2, :]
        PTb = st16b[0]
        ptsem = P.dsem("ptsem")
        P.dma("pool", ptsem, out=PT, in_=pT.rearrange("(c p) t -> p c t", p=128), writes=[st16b[0], st16b[1]])
        hdst = h4 if final else hout
        pj = [0]

        def epi_ple(j, bset):
            jj = j % 4
            if jj == 0:
                wple_cur[0] = load_w(cx, w_ple, 0, 2, j * 128, 512)
            wv, wb = wple_cur[0]
            for h in range(2):
                for k in range(2):
                    P.op("pe", "matmul", reads=[wb, PTb], writes=[cx.psb[4 + h]], signal=(k == 1 and h == 1),
                         out=cx.ps[4 + h][:, :], lhsT=wv[:, k, jj * 128:(jj + 1) * 128],
                         rhs=PT[:, k, h * 512:(h + 1) * 512], start=(k == 0), stop=(k == 1))
            s, s2 = n32(), n32()
            for h in range(2):
                hs = slice(h * 512, (h + 1) * 512)
                P.op("dve", "tensor_tensor", reads=[cx.psb[bset[h]], rstdb], writes=[st32b[s]],
                     out=st32[:, s, hs], in0=cx.ps[bset[h]][:, :], in1=rstd[:, hs], op=ALU.mult)
            P.op("act", "activation", reads=[st32b[s]], writes=[st32b[s]],
                 out=st32[:, s, :], in_=st32[:, s, :], func=AF.Sigmoid)
            for h in range(2):
                hs = slice(h * 512, (h + 1) * 512)
                P.op("dve", "tensor_tensor", reads=[cx.psb[4 + h], st32b[s]], writes=[st32b[s]],
                     out=st32[:, s, hs], in0=cx.ps[4 + h][:, :], in1=st32[:, s, hs], op=ALU.mult)
            P.dma("sp", st32s[s2], out=st32[:, s2, :], in_=h3[j * 128:(j + 1) * 128, :],
                  reads=[h3b[j]], writes=[st32b[s2]])
            P.op("dve", "tensor_tensor", reads=[st32b[s], st32b[s2]], writes=[st32b[s2]],
                 out=st32[:, s2, :], in0=st32[:, s, :], in1=st32[:, s2, :], op=ALU.add)
            P.dma("sp", st32s[s2], out=hdst[j * 128:(j + 1) * 128, :], in_=st32[:, s2, :],
                  reads=[st32b[s2]], writes=[h4b[j]])
            if final:
                q = sqc[0] % 2
                sqc[0] += 1
                P.op("act", "activation", reads=[st32b[s2]], writes=[sqb[q]],
                     out=sq[:, q, :], in_=st32[:, s2, :], func=AF.Square)
                for h in range(2):
                    P.op("pe", "matmul", reads=[sqb[q], cx.onesb], writes=[cx.psb[6 + h]],
                         signal=(h == 1), out=cx.ps[6 + h][:, :], lhsT=cx.ones[:, :],
                         rhs=sq[:, q, h * 512:(h + 1) * 512], start=(j == 0), stop=(j == KC - 1))

        wple_cur = [None]
        gemm_fm(cx, R2, R2b, KC, w_pg, 0, D, epi_ple, banks)
        if final:
            make_rstd(True, (6, 7))
            for j in range(KC):
                s = n32()
                P.dma("sp", st32s[s], out=st32[:, s, :], in_=h4[j * 128:(j + 1) * 128, :],
                      reads=[h4b[j]], writes=[st32b[s]])
                P.op("dve", "scalar_tensor_tensor", reads=[st32b[s], rstdb, gb], writes=[st32b[s]],
                     out=st32[:, s, :], in0=st32[:, s, :], scalar=g_sb[:, 2 * KC + j:2 * KC + j + 1],
                     in1=rstd[:, :], op0=ALU.mult, op1=ALU.mult)
                P.dma("sp", st32s[s], out=hout[j * 128:(j + 1) * 128, :], in_=st32[:, s, :], reads=[st32b[s]])
        P.finalize()
    return nc


def build_L2():
    import math
    nc = bass.Bass("TRN2", target_bir_lowering=False)
    dt_ = nc.dram_tensor
    qT = dt_("qT", [2048, T], BF16, kind="ExternalInput").ap()
    kTf = dt_("kTf", [2048, S], BF16, kind="ExternalInput").ap()
    vhm = dt_("vhm", [16, 128, 16, 128], BF16, kind="ExternalInput").ap()
    maskd = dt_("mask", [128, 128], F32, kind="ExternalInput").ap()
    identd = dt_("ident", [128, 128], BF16, kind="ExternalInput").ap()
    uTg = dt_("uTg", [1024, S], F32, kind="ExternalInput").ap()
    lamd = dt_("lam", [128, 96], F32, kind="ExternalInput").ap()
    BTd = dt_("BT", [32, 2, 32, 128], F32, kind="ExternalInput").ap()
    Cd = dt_("Cm", [128, 2, 32, 32], F32, kind="ExternalInput").ap()
    dskd = dt_("dsk", [32, 32], F32, kind="ExternalInput").ap()
    attnT = dt_("attnT", [2048, T], BF16, kind="ExternalOutput").ap()
    yg = dt_("yg", [1024, S], F32, kind="ExternalOutput").ap()
    with contextlib.ExitStack() as stack:
        P = Prog(nc, stack)
        ps = [P.psum("ps%d" % i, [128, 512], F32) for i in range(4)]
        pT = [P.psum("pT%d" % i, [128, 1024], BF16) for i in range(2)]
        ps6 = P.psum("ps6", [128, 512], F32)
        ps7 = P.psum("ps7", [128, 512], F32)
        ps7b = P.buf()
        psb = [P.buf() for _ in range(4)]
        pTb = [P.buf(), P.buf()]
        ps6b = P.buf()
        csem = P.dsem("csem")
        mask = P.sb("mask_sb", [128, 128], F32); maskb = P.buf()
        ident = P.sb("ident_sb", [128, 128], BF16); identb = P.buf()
        P.dma("sp", csem, out=mask[:, :], in_=maskd[:, :], writes=[maskb])
        csem_i = P.dsem("csem_i")
        P.dma("sp", csem_i, out=ident[:, :], in_=identd[:, :], writes=[identb])
        ones32 = P.sb("ones32", [128, S], F32); ones32b = P.buf()
        P.op("dve", "memset", writes=[ones32b], ap=ones32[:, :], constant=1.0)

        lam = P.sb("lam_sb", [128, 96], F32); lamb = P.buf()
        csem_l = P.dsem("csem_l")
        P.dma("sp", csem_l, out=lam[:, :], in_=lamd[:, :], writes=[lamb])
        BT = P.sb("BT_sb", [32, 2, 32, 128], BF16); BTb = P.buf()
        bsem = P.dsem("bsem")
        P.dma("pool", bsem, out=BT[:, :, :, :], in_=BTd[:, :, :, :], writes=[BTb])
        Cm = P.sb("Cm_sb", [128, 2, 32, 32], F32); Cmb = P.buf()
        c2sem = P.dsem("c2sem")
        P.dma("sp", c2sem, out=Cm[:, :, :, :], in_=Cd[:, :, :, :], writes=[Cmb])
        dsk = P.sb("dsk_sb", [32, 32], F32); dskb = P.buf()
        d2sem = P.dsem("d2sem")
        P.dma("sp", d2sem, out=dsk[:, :], in_=dskd[:, :], writes=[dskb])
        pr = P.sb("pr", [128, 16, 32], F32); prb = P.buf()
        LR, LI, LD = lam[:, 0:32], lam[:, 32:64], lam[:, 64:96]
        X = lambda k: pr[:, k, :]
        def dv(method, **kw):
            P.op("dve", method, reads=[prb, lamb], writes=[prb], **kw)
        def ac(**kw):
            P.op("act", "activation", reads=[prb, lamb], writes=[prb], **kw)
        TWO_PI = 2.0 * math.pi
        MAGIC = 12582912.0
        ac(out=X(0), in_=LD, func=AF.Exp)
        dv("tensor_tensor", out=X(1), in0=LR, in1=X(0), op=ALU.mult)
        ac(out=X(2), in_=X(1), func=AF.Exp)
        dv("tensor_tensor", out=X(3), in0=LI, in1=X(0), op=ALU.mult)
        dv("tensor_scalar", out=X(3), in0=X(3), scalar1=1.0 / TWO_PI, scalar2=None, op0=ALU.mult)
        dv("tensor_scalar", out=X(4), in0=X(3), scalar1=MAGIC, scalar2=None, op0=ALU.add)
        dv("tensor_scalar", out=X(4), in0=X(4), scalar1=MAGIC, scalar2=None, op0=ALU.subtract)
        dv("tensor_tensor", out=X(4), in0=X(3), in1=X(4), op=ALU.subtract)
        ac(out=X(5), in_=X(4), func=AF.Sin, scale=TWO_PI)
        dv("tensor_scalar", out=X(6), in0=X(3), scalar1=0.25, scalar2=None, op0=ALU.add)
        dv("tensor_scalar", out=X(4), in0=X(6), scalar1=MAGIC, scalar2=None, op0=ALU.add)
        dv("tensor_scalar", out=X(4), in0=X(4), scalar1=MAGIC, scalar2=None, op0=ALU.subtract)
        dv("tensor_tensor", out=X(4), in0=X(6), in1=X(4), op=ALU.subtract)
        ac(out=X(6), in_=X(4), func=AF.Sin, scale=TWO_PI)
        COS, SIN, RHO = X(6), X(5), X(2)
        dv("tensor_tensor", out=X(7), in0=RHO, in1=COS, op=ALU.mult)
        dv("tensor_scalar", out=X(7), in0=X(7), scalar1=-1.0, scalar2=None, op0=ALU.add)
        dv("tensor_tensor", out=X(8), in0=RHO, in1=SIN, op=ALU.mult)
        dv("tensor_tensor", out=X(9), in0=LR, in1=LR, op=ALU.mult)
        dv("tensor_tensor", out=X(10), in0=LI, in1=LI, op=ALU.mult)
        dv("tensor_tensor", out=X(9), in0=X(9), in1=X(10), op=ALU.add)
        dv("reciprocal", out=X(9), in_=X(9))
        dv("tensor_tensor", out=X(10), in0=X(7), in1=LR, op=ALU.mult)
        dv("tensor_tensor", out=X(11), in0=X(8), in1=LI, op=ALU.mult)
        dv("tensor_tensor", out=X(10), in0=X(10), in1=X(11), op=ALU.add)
        dv("tensor_tensor", out=X(10), in0=X(10), in1=X(9), op=ALU.mult)
        dv("tensor_tensor", out=X(11), in0=X(8), in1=LR, op=ALU.mult)
        dv("tensor_tensor", out=X(12), in0=X(7), in1=LI, op=ALU.mult)
        dv("tensor_tensor", out=X(11), in0=X(11), in1=X(12), op=ALU.subtract)
        dv("tensor_tensor", out=X(11), in0=X(11), in1=X(9), op=ALU.mult)
        dv("tensor_scalar", out=X(12), in0=X(11), scalar1=-1.0, scalar2=None, op0=ALU.mult)
        KRE, KIM, NKIM = X(10), X(11), X(12)

        Ec = P.sb("Ec", [128, S], F32); Es = P.sb("Es", [128, S], F32); Eb = P.buf()
        rhoT = P.sb("rhoT", [128, T], F32); rhoTb = P.buf()
        tmpA = P.sb("tmpA", [128, T], F32); tmpAb = P.buf()
        tmpB = P.sb("tmpB", [128, T], F32); tmpBb = P.buf()
        Zr = P.sb("Zr", [128, 2, T], F32); Zi = P.sb("Zi", [128, 2, T], F32); Zrb = [P.buf(), P.buf()]; Zib = [P.buf(), P.buf()]
        xr = P.sb("xr", [128, T], BF16); xi = P.sb("xi", [128, T], BF16); xrb = P.buf(); xib = P.buf()
        ut = P.sb("ut", [32, 2, S], F32); utb = [P.buf(), P.buf()]; utsem = [P.dsem("uts0"), P.dsem("uts1")]
        ub = P.sb("ub", [32, 2, S], BF16); ubb = [P.buf(), P.buf()]
        Ct = P.sb("Ct", [128, 2, 2, 32], BF16); Ctb = [P.buf(), P.buf()]
        ctmp = P.sb("ctmp", [128, 32], F32); ctmpb = P.buf()
        yo = P.sb("yo", [32, 2, S], F32); yob = [P.buf(), P.buf()]; yosem = [P.dsem("yos0"), P.dsem("yos1")]
        def ssm_tile(i):
            s = i % 2
            P.dma("sp", utsem[s], out=ut[:, s, :], in_=uTg[i * 32:(i + 1) * 32, :], writes=[utb[s]])
            P.op("act", "activation", reads=[utb[s]], writes=[ubb[s]], out=ub[:, s, :], in_=ut[:, s, :], func=AF.Copy)
            P.op("dve", "tensor_scalar", reads=[Cmb, prb], writes=[ctmpb], out=ctmp[:, :], in0=Cm[:, 0, i, :],
                 scalar1=KRE[:, i:i + 1], scalar2=None, op0=ALU.mult)
            P.op("dve", "scalar_tensor_tensor", reads=[Cmb, prb, ctmpb], writes=[Ctb[s]], out=Ct[:, s, 0, :],
                 in0=Cm[:, 1, i, :], scalar=NKIM[:, i:i + 1], in1=ctmp[:, :], op0=ALU.mult, op1=ALU.add)
            P.op("dve", "tensor_scalar", reads=[Cmb, prb], writes=[ctmpb], out=ctmp[:, :], in0=Cm[:, 1, i, :],
                 scalar1=KRE[:, i:i + 1], scalar2=None, op0=ALU.mult)
            P.op("dve", "scalar_tensor_tensor", reads=[Cmb, prb, ctmpb], writes=[Ctb[s]], out=Ct[:, s, 1, :],
                 in0=Cm[:, 0, i, :], scalar=NKIM[:, i:i + 1], in1=ctmp[:, :], op0=ALU.mult, op1=ALU.subtract)
            P.op("dve", "tensor_copy", reads=[prb], writes=[Eb], out=Ec[:, 0:1], in_=COS[:, i:i + 1])
            P.op("dve", "tensor_copy", reads=[prb], writes=[Eb], out=Es[:, 0:1], in_=SIN[:, i:i + 1])
            n = 1
            while n < S:
                cn, sn = Ec[:, n - 1:n], Es[:, n - 1:n]
                P.op("dve", "tensor_scalar", reads=[Eb], writes=[tmpAb], out=tmpA[:, 0:n], in0=Es[:, 0:n], scalar1=sn,
                     scalar2=None, op0=ALU.mult)
                P.op("dve", "tensor_scalar", reads=[Eb], writes=[tmpBb], out=tmpB[:, 0:n], in0=Ec[:, 0:n], scalar1=sn,
                     scalar2=None, op0=ALU.mult)
                P.op("dve", "scalar_tensor_tensor", reads=[Eb, tmpAb], writes=[Eb], out=Ec[:, n:2 * n], in0=Ec[:, 0:n],
                     scalar=cn, in1=tmpA[:, 0:n], op0=ALU.mult, op1=ALU.subtract)
                P.op("dve", "scalar_tensor_tensor", reads=[Eb, tmpBb], writes=[Eb], out=Es[:, n:2 * n], in0=Es[:, 0:n],
                     scalar=cn, in1=tmpB[:, 0:n], op0=ALU.mult, op1=ALU.add)
                n *= 2
            P.op("dve", "tensor_scalar", reads=[ones32b, prb], writes=[rhoTb], out=rhoT[:, :], in0=ones32[:, 0:T],
                 scalar1=RHO[:, i:i + 1], scalar2=None, op0=ALU.mult)
            for hh in range(2):
                t0 = hh * T
                for c in range(2):
                    P.op("pe", "matmul", reads=[BTb, ubb[s]], writes=[psb[c]], signal=False,
                         out=ps[c][:, :], lhsT=BT[:, 0, i, :], rhs=ub[:, s, t0 + c * 512:t0 + (c + 1) * 512],
                         start=True, stop=True)
                    P.op("pe", "matmul", reads=[BTb, ubb[s]], writes=[psb[2 + c]], signal=(c == 1),
                         out=ps[2 + c][:, :], lhsT=BT[:, 1, i, :], rhs=ub[:, s, t0 + c * 512:t0 + (c + 1) * 512],
                         start=True, stop=True)
                for c in range(2):
                    cs = slice(c * 512, (c + 1) * 512)
                    gs_ = slice(t0 + c * 512, t0 + (c + 1) * 512)
                    P.op("dve", "tensor_tensor", reads=[psb[c], Eb], writes=[tmpAb], out=tmpA[:, cs], in0=ps[c][:, :],
                         in1=Ec[:, gs_], op=ALU.mult)
                    P.op("dve", "tensor_tensor", reads=[psb[2 + c], Eb], writes=[tmpBb], out=tmpB[:, cs],
                         in0=ps[2 + c][:, :], in1=Es[:, gs_], op=ALU.mult)
                    P.op("dve", "tensor_tensor", reads=[tmpAb, tmpBb], writes=[Zrb[hh]], out=Zr[:, hh, cs],
                         in0=tmpA[:, cs], in1=tmpB[:, cs], op=ALU.add)
                    P.op("dve", "tensor_tensor", reads=[psb[2 + c], Eb], writes=[tmpAb], out=tmpA[:, cs],
                         in0=ps[2 + c][:, :], in1=Ec[:, gs_], op=ALU.mult)
                    P.op("dve", "tensor_tensor", reads=[psb[c], Eb], writes=[tmpBb], out=tmpB[:, cs], in0=ps[c][:, :],
                         in1=Es[:, gs_], op=ALU.mult)
                    P.op("dve", "tensor_tensor", reads=[tmpAb, tmpBb], writes=[Zib[hh]], out=Zi[:, hh, cs],
                         in0=tmpA[:, cs], in1=tmpB[:, cs], op=ALU.subtract)
                ir = 0.0 if hh == 0 else Zr[:, 0, T - 1:T]
                ii = 0.0 if hh == 0 else Zi[:, 0, T - 1:T]
                P.op("dve", "tensor_tensor_scan", reads=[rhoTb, Zrb[hh]] + ([Zrb[0]] if hh else []), writes=[Zrb[hh]],
                     out=Zr[:, hh, :], data0=rhoT[:, :], data1=Zr[:, hh, :], initial=ir, op0=ALU.mult, op1=ALU.add)
                P.op("dve", "tensor_tensor_scan", reads=[rhoTb, Zib[hh]] + ([Zib[0]] if hh else []), writes=[Zib[hh]],
                     out=Zi[:, hh, :], data0=rhoT[:, :], data1=Zi[:, hh, :], initial=ii, op0=ALU.mult, op1=ALU.add)
                gsl = slice(t0, t0 + T)
                P.op("dve", "tensor_tensor", reads=[Zrb[hh], Eb], writes=[tmpAb], out=tmpA[:, :], in0=Zr[:, hh, :],
                     in1=Ec[:, gsl], op=ALU.mult)
                P.op("dve", "tensor_tensor", reads=[Zib[hh], Eb], writes=[tmpBb], out=tmpB[:, :], in0=Zi[:, hh, :],
                     in1=Es[:, gsl], op=ALU.mult)
                P.op("dve", "tensor_tensor", reads=[tmpAb, tmpBb], writes=[xrb], out=xr[:, :], in0=tmpA[:, :],
                     in1=tmpB[:, :], op=ALU.subtract)
                P.op("dve", "tensor_tensor", reads=[Zib[hh], Eb], writes=[tmpAb], out=tmpA[:, :], in0=Zi[:, hh, :],
                     in1=Ec[:, gsl], op=ALU.mult)
                P.op("dve", "tensor_tensor", reads=[Zrb[hh], Eb], writes=[tmpBb], out=tmpB[:, :], in0=Zr[:, hh, :],
                     in1=Es[:, gsl], op=ALU.mult)
                P.op("dve", "tensor_tensor", reads=[tmpAb, tmpBb], writes=[xib], out=xi[:, :], in0=tmpA[:, :],
                     in1=tmpB[:, :], op=ALU.add)
                for c in range(2):
                    cs = slice(c * 512, (c + 1) * 512)
                    P.op("pe", "matmul", reads=[Ctb[s], xrb], writes=[ps7b], signal=False,
                         out=ps7[0:32, :], lhsT=Ct[:, s, 0, :], rhs=xr[:, cs], start=True, stop=False)
                    P.op("pe", "matmul", reads=[Ctb[s], xib], writes=[ps7b], signal=True,
                         out=ps7[0:32, :], lhsT=Ct[:, s, 1, :], rhs=xi[:, cs], start=False, stop=True)
                    gs_ = slice(t0 + c * 512, t0 + (c + 1) * 512)
                    P.op("dve", "scalar_tensor_tensor", reads=[utb[s], dskb, ps7b], writes=[yob[s]],
                         out=yo[:, s, gs_], in0=ut[:, s, gs_], scalar=dsk[:, i:i + 1], in1=ps7[0:32, :],
                         op0=ALU.mult, op1=ALU.add)
                yield
            P.dma("sp", yosem[s], out=yg[i * 32:(i + 1) * 32, :], in_=yo[:, s, :], reads=[yob[s]])

        qh = P.sb("qh", [128, 2, T], BF16); kh = P.sb("kh", [128, 2, S], BF16); vh = P.sb("vh", [128, 2, 16, 128], BF16)
        hb = [P.buf(), P.buf()]; hsem = [P.dsem("hs0"), P.dsem("hs1")]
        Bt = P.sb("Bt", [128, S], F32); Lt = P.sb("Lt", [128, S], F32); Btb = P.buf(); Ltb = P.buf()
        Pt = P.sb("Pt", [128, S], F32); Ptb = P.buf()
        Wt = P.sb("Wt", [128, S], BF16); Wtb = P.buf()
        WT = P.sb("WT", [128, 16, 128], BF16); WTb = P.buf()
        ao = P.sb("ao", [128, 2, T], BF16); aob = [P.buf(), P.buf()]; aosem = [P.dsem("aos0"), P.dsem("aos1")]
        def attn_head(h):
            s = h % 2
            P.dma("sp", hsem[s], out=qh[:, s, :], in_=qT[h * 128:(h + 1) * 128, :], writes=[hb[s]])
            P.dma("sp", hsem[s], out=kh[:, s, :], in_=kTf[h * 128:(h + 1) * 128, :], writes=[hb[s]])
            P.dma("sp", hsem[s], out=vh[:, s, :, :], in_=vhm[h], writes=[hb[s]])
            for i in range(8):
                nblk = 9 + i
                n = nblk * 128
                nch = (n + 511) // 512
                for c in range(nch):
                    w_ = min(512, n - c * 512)
                    P.op("pe", "matmul", reads=[hb[s]], writes=[psb[c]], signal=True,
                         out=ps[c][:, 0:w_], lhsT=qh[:, s, i * 128:(i + 1) * 128], rhs=kh[:, s, c * 512:c * 512 + w_],
                         start=True, stop=True)
                    P.op("act", "activation", reads=[psb[c]], writes=[Btb], out=Bt[:, c * 512:c * 512 + w_],
                         in_=ps[c][:, 0:w_], func=AF.Sigmoid)
                P.op("act", "activation", reads=[Btb], writes=[Ltb], out=Lt[:, 0:n], in_=Bt[:, 0:n], func=AF.Ln,
                     scale=-1.0, bias=ones32[:, 0:1])
                P.op("dve", "tensor_tensor", reads=[Ltb, maskb], writes=[Ltb], out=Lt[:, n - 128:n], in0=Lt[:, n - 128:n],
                     in1=mask[:, :], op=ALU.mult)
                P.op("dve", "tensor_tensor", reads=[Btb, maskb], writes=[Btb], out=Bt[:, n - 128:n], in0=Bt[:, n - 128:n],
                     in1=mask[:, :], op=ALU.mult)
                P.op("dve", "tensor_tensor_scan", reads=[ones32b, Ltb], writes=[Ptb], out=Pt[:, 0:n], data0=ones32[:, 0:n],
                     data1=Lt[:, 0:n], initial=0.0, op0=ALU.mult, op1=ALU.add)
                P.op("act", "activation", reads=[Ptb], writes=[Ltb], out=Lt[:, 0:n], in_=Pt[:, 0:n], func=AF.Exp,
                     scale=-1.0, bias=Pt[:, n - 1:n])
                P.op("dve", "tensor_tensor", reads=[Btb, Ltb], writes=[Wtb], out=Wt[:, 0:n], in0=Bt[:, 0:n], in1=Lt[:, 0:n],
                     op=ALU.mult)
                for blk in range(nblk):
                    tb = blk // 8
                    P.op("pe", "transpose", reads=[Wtb, identb], writes=[pTb[tb]], signal=(blk == nblk - 1 or blk == 7),
                         out=pT[tb][:, (blk % 8) * 128:(blk % 8 + 1) * 128], in_=Wt[:, blk * 128:(blk + 1) * 128],
                         identity=ident[:, :])
                P.op("act", "activation", reads=[pTb[0]], writes=[WTb], out=WT[:, 0:8, :],
                     in_=pT[0][:, :].rearrange("p (b q) -> p b q", b=8), func=AF.Copy)
                nb2 = nblk - 8
                P.op("dve", "tensor_copy", reads=[pTb[1]], writes=[WTb], out=WT[:, 8:8 + nb2, :],
                     in_=pT[1][:, 0:nb2 * 128].rearrange("p (b q) -> p b q", b=nb2))
                for blk in range(nblk):
                    P.op("pe", "matmul", reads=[hb[s], WTb], writes=[ps6b], signal=(blk == nblk - 1),
                         out=ps6[:, 0:128], lhsT=vh[:, s, blk + (7 - i), :] if False else vh[:, s, blk, :],
                         rhs=WT[:, blk, :], start=(blk == 0), stop=(blk == nblk - 1))
                P.op("act", "activation", reads=[ps6b], writes=[aob[s]], out=ao[:, s, i * 128:(i + 1) * 128],
                     in_=ps6[:, 0:128], func=AF.Copy)
                yield
            P.dma("sp", aosem[s], out=attnT[h * 128:(h + 1) * 128, :], in_=ao[:, s, :], reads=[aob[s]])
        def attn_gen():
            for h_ in range(16):
                yield from attn_head(h_)

        def ssm_gen():
            for i_ in range(32):
                yield from ssm_tile(i_)

        ga, gs_ = attn_gen(), ssm_gen()
        for step in range(64):
            for _ in range(2):
                next(ga, None)
            next(gs_, None)
        for _ in ga:
            pass
        for _ in gs_:
            pass
        P.finalize()
    return nc


def ssm_params(lam_re, lam_im, log_dt, b_re, b_im, c_re, c_im, d_skip, half):
    g0 = 64 * half
    G = np.arange(g0, g0 + 64).reshape(32, 2)
    lam = np.zeros((128, 96), np.float32)
    lam[:, 0:32] = lam_re[G].transpose(1, 2, 0).reshape(128, 32)
    lam[:, 32:64] = lam_im[G].transpose(1, 2, 0).reshape(128, 32)
    lam[:, 64:96] = np.repeat(log_dt[G].T[:, None, :], 64, axis=1).reshape(128, 32)
    BT = np.zeros((32, 2, 32, 128), np.float32)
    Cm = np.zeros((128, 2, 32, 32), np.float32)
    for gl in range(2):
        for ri, (b, c) in enumerate(((b_re, c_re), (b_im, c_im))):
            BT[gl * 16:(gl + 1) * 16, ri, :, gl * 64:(gl + 1) * 64] = b[G[:, gl]].transpose(2, 0, 1)
            Cm[gl * 64:(gl + 1) * 64, ri, :, gl * 16:(gl + 1) * 16] = c[G[:, gl]].transpose(2, 0, 1)
    dsk = d_skip.reshape(128, 16)[G].transpose(1, 2, 0).reshape(32, 32)
    return {"lam": lam, "BT": BT, "Cm": Cm, "dsk": np.ascontiguousarray(dsk)}

def attn_consts():
    return {"mask": np.tril(np.ones((128, 128), np.float32), -1), "ident": np.eye(128, dtype=np.float32).astype(bf)}

def v_headmajor(vfull):
    return np.ascontiguousarray(vfull.reshape(16, 128, 16, 128).transpose(2, 1, 0, 3))


NCORES = 8


def _lay(g):
    return np.ascontiguousarray(np.asarray(g, np.float32).reshape(32, 128).T)


def _run(nc, in_maps):
    res = run_bass_kernel_spmd(nc, in_maps, core_ids=list(range(NCORES)))
    return res.results


def kernel(x, p, g_mix, w_in, w_br_attn, lam_re, lam_im, log_dt, b_re, b_im, c_re, c_im, d_skip, w_glu,
           w_br_ssm, w_o, g_mlp, w_ff1, w_ff2, g_ple, w_ple_gate, w_ple, g_final):
    A = lambda a: np.asarray(a, dtype=np.float32)
    x, p = A(x), A(p)
    depth = x.shape[0] and A(g_mix).shape[0]
    cores = [(c // 2, c % 2) for c in range(NCORES)]
    hT = [np.ascontiguousarray(x[b, hf * T:(hf + 1) * T, :].T) for (b, hf) in cores]
    consts = attn_consts()
    for l in range(depth):
        nc1 = build_L1()
        wl = np.ascontiguousarray(A(w_in[l]))
        gm = _lay(g_mix[l])
        r1 = _run(nc1, [{"hT": hT[c], "gmix": gm, "w_in": wl} for c in range(NCORES)])
        del wl
        in2 = []
        for c, (b, hf) in enumerate(cores):
            c0, c1 = 2 * b, 2 * b + 1
            if hf == 1:
                kTf = np.concatenate([r1[c0]["kT"], r1[c1]["kT"]], axis=1)
                vfull = np.concatenate([r1[c0]["v"], r1[c1]["v"]], axis=0)
            else:
                kTf = np.concatenate([np.zeros_like(r1[c0]["kT"]), r1[c0]["kT"]], axis=1)
                vfull = np.concatenate([np.zeros_like(r1[c0]["v"]), r1[c0]["v"]], axis=0)
            ufull = np.concatenate([r1[c0]["uT"], r1[c1]["uT"]], axis=1)
            d = {"qT": r1[c]["qT"], "kTf": np.ascontiguousarray(kTf), "vhm": v_headmajor(vfull),
                 "uTg": np.ascontiguousarray(ufull[hf * 1024:(hf + 1) * 1024, :])}
            d.update(consts)
            d.update(ssm_params(A(lam_re[l]), A(lam_im[l]), A(log_dt[l]), A(b_re[l]), A(b_im[l]), A(c_re[l]),
                                A(c_im[l]), A(d_skip[l]), hf))
            in2.append(d)
        nc2 = build_L2()
        r2 = _run(nc2, in2)
        del in2
        final = (l == depth - 1)
        nc3 = build_L3(final)
        gvec = np.concatenate([_lay(g_mlp[l]), _lay(g_ple[l]), _lay(g_final)], axis=1)
        W = {"w_bra": A(w_br_attn[l]), "w_glu": A(w_glu[l]), "w_brs": A(w_br_ssm[l]), "w_o": A(w_o[l]),
             "w_ff1": A(w_ff1[l]), "w_ff2": A(w_ff2[l]), "w_pg": A(w_ple_gate[l]), "w_ple": A(w_ple[l])}
        W = {k_: np.ascontiguousarray(v_) for k_, v_ in W.items()}
        in3 = []
        for c, (b, hf) in enumerate(cores):
            c0, c1 = 2 * b, 2 * b + 1
            yfull = np.concatenate([r2[c0]["yg"], r2[c1]["yg"]], axis=0)
            d = {"attnT": r2[c]["attnT"], "yT": np.ascontiguousarray(yfull[:, hf * T:(hf + 1) * T]),
                 "gT": r1[c]["gT"], "hT": hT[c], "pT": np.ascontiguousarray(p[l, b, hf * T:(hf + 1) * T, :].T),
                 "gvec": gvec}
            d.update(W)
            in3.append(d)
        r3 = _run(nc3, in3)
        del in3, W, r1, r2
        hT = [r3[c]["hout"] for c in range(NCORES)]
    out = np.zeros(x.shape, np.float32)
    for c, (b, hf) in enumerate(cores):
        out[b, hf * T:(hf + 1) * T, :] = hT[c].T
    return out
```
